# Optimizing a Trainium2 kernel written in Bass

```python
import math
import jax, jax.numpy as jnp
from jax import lax
import numpy as np

D_MODEL = 1024
BATCH = 8
SEQ = 2048
DEPTH = 4
DEC_BATCH = 128
DEC_SEQ = 1
PAST_LEN = 16384
PAGE_SIZE = 128

N_AB = (DEPTH + 1) // 2
N_C = DEPTH // 2
D_S5 = D_MODEL // 2
S5_GROUP = 16
S5_GROUPS = D_S5 // S5_GROUP
S5_STATE = 64
D_HG = D_MODEL // 2
HG_DK = 128
HG_HEADS = D_HG // HG_DK
HG_DV = D_HG // HG_HEADS
HG_CHUNK = 16
D_AB_IN = D_S5 + 4 * D_HG
D_CONF = D_MODEL
CONF_W = 31
N_MEM = 256
X_HEADS = 4
X_HEAD_DIM = D_MODEL // X_HEADS
D_FF = ((8 * D_MODEL // 3 + 127) // 128) * 128
FFN_W = 3
EPS = 1e-6
N_NORMS = 7
NG_MIX_PRE, NG_MIX_POST, NG_X_PRE, NG_X_POST, NG_FFN_PRE, NG_FFN_POST, NG_MEM = range(N_NORMS)

kernel_name = 'hybrid_s5_hgrn2_conformer_decoder_step'


def _rms_norm(x, g):
    xf = x.astype(jnp.float32)
    y = xf * lax.rsqrt(jnp.mean(xf * xf, axis=-1, keepdims=True) + EPS)
    return (y * g.astype(jnp.float32)).astype(x.dtype)


def _layer_norm(x, g, b):
    xf = x.astype(jnp.float32)
    mu = jnp.mean(xf, axis=-1, keepdims=True)
    xc = xf - mu
    y = xc * lax.rsqrt(jnp.mean(xc * xc, axis=-1, keepdims=True) + EPS)
    return (y * g.astype(jnp.float32) + b.astype(jnp.float32)).astype(x.dtype)


def _causal_dwconv(u, buf, w, b):
    n_tap = w.shape[0]
    ext = jnp.concatenate([buf.astype(u.dtype), u], axis=1)
    out = lax.conv_general_dilated(ext, w.astype(u.dtype)[:, None, :], (1,), 'VALID',
                                   dimension_numbers=('NWC', 'WIO', 'NWC'),
                                   feature_group_count=u.shape[-1])
    return out + b.astype(u.dtype), ext[:, ext.shape[1] - (n_tap - 1):]


def _s5(u, h_re, h_im, lam_re, lam_im, b_re, b_im, c_re, c_im, d, log_dt, w_glu, b_glu):
    f32 = jnp.float32
    n, l, _ = u.shape
    uf = u.astype(f32).reshape(n, l, S5_GROUPS, S5_GROUP)
    dt = jnp.exp(log_dt.astype(f32))[:, None]
    lr, li = lam_re.astype(f32), lam_im.astype(f32)
    mag = jnp.exp(lr * dt)
    ar, ai = mag * jnp.cos(li * dt), mag * jnp.sin(li * dt)
    den = lr * lr + li * li
    er = ((ar - 1.0) * lr + ai * li) / den
    ei = (ai * lr - (ar - 1.0) * li) / den
    br, bi = b_re.astype(f32), b_im.astype(f32)
    bbr = er[..., None] * br - ei[..., None] * bi
    bbi = er[..., None] * bi + ei[..., None] * br
    bu_r = jnp.einsum('gpc,nlgc->nlgp', bbr, uf)
    bu_i = jnp.einsum('gpc,nlgc->nlgp', bbi, uf)
    hr0, hi0 = h_re.astype(f32), h_im.astype(f32)
    bu_r = bu_r.at[:, 0].add(ar * hr0 - ai * hi0)
    bu_i = bu_i.at[:, 0].add(ar * hi0 + ai * hr0)
    a_r = jnp.broadcast_to(ar, bu_r.shape)
    a_i = jnp.broadcast_to(ai, bu_i.shape)

    def combine(e1, e2):
        a1r, a1i, b1r, b1i = e1
        a2r, a2i, b2r, b2i = e2
        return (a1r * a2r - a1i * a2i, a1r * a2i + a1i * a2r,
                a2r * b1r - a2i * b1i + b2r, a2r * b1i + a2i * b1r + b2i)

    _, _, xr, xi = lax.associative_scan(combine, (a_r, a_i, bu_r, bu_i), axis=1)
    y = (jnp.einsum('gcp,nlgp->nlgc', c_re.astype(f32), xr)
         - jnp.einsum('gcp,nlgp->nlgc', c_im.astype(f32), xi)).reshape(n, l, D_S5)
    y = y + d.astype(f32) * u.astype(f32)
    z = jax.nn.gelu(y)
    out = z * jax.nn.sigmoid(z @ w_glu.astype(f32) + b_glu.astype(f32))
    return out, xr[:, -1], xi[:, -1]


def _hgrn2(q, f_logit, i_in, g_out, s0, lb, gnorm):
    f32 = jnp.float32
    n, l, _ = q.shape
    f = lb + (1.0 - lb) * jax.nn.sigmoid(f_logit.astype(f32))
    log_f = jnp.log(f).reshape(n, l, HG_HEADS, HG_DK)
    k = (1.0 - f).reshape(n, l, HG_HEADS, HG_DK)
    qf = jax.nn.silu(q.astype(f32)).reshape(n, l, HG_HEADS, HG_DK)
    v = i_in.astype(f32).reshape(n, l, HG_HEADS, HG_DV)
    n_chunk = -(-l // HG_CHUNK)
    pad = n_chunk * HG_CHUNK - l

    def to_chunks(t):
        t = jnp.pad(t, ((0, 0), (0, pad), (0, 0), (0, 0)))
        return t.reshape(n, n_chunk, HG_CHUNK, HG_HEADS, -1).transpose(1, 0, 3, 2, 4)

    causal = jnp.tril(jnp.ones((HG_CHUNK, HG_CHUNK), bool))[:, :, None]

    def step(s, inp):
        qc, lc, kc, vc = inp
        b = jnp.cumsum(lc, axis=2)
        diff = b[:, :, :, None, :] - b[:, :, None, :, :]
        decay = jnp.exp(jnp.where(causal, diff, -jnp.inf))
        att = jnp.einsum('nhtd,nhsd,nhtsd->nhts', qc, kc, decay)
        o = (jnp.einsum('nhts,nhsv->nhtv', att, vc)
             + jnp.einsum('nhtd,nhdv->nhtv', qc * jnp.exp(b), s))
        b_end = b[:, :, -1:, :]
        s_new = (jnp.exp(b_end[:, :, 0, :])[..., None] * s
                 + jnp.einsum('nhsd,nhsv->nhdv', kc * jnp.exp(b_end - b), vc))
        return s_new, o

    s_fin, o = lax.scan(step, s0.astype(f32),
                        (to_chunks(qf), to_chunks(log_f), to_chunks(k), to_chunks(v)))
    o = o.transpose(1, 0, 3, 2, 4).reshape(n, n_chunk * HG_CHUNK, HG_HEADS, HG_DV)[:, :l]
    o = o * lax.rsqrt(jnp.mean(o * o, axis=-1, keepdims=True) + EPS) * gnorm.astype(f32).reshape(HG_HEADS, HG_DV)
    o = o * jax.nn.sigmoid(g_out.astype(f32)).reshape(n, l, HG_HEADS, HG_DV)
    return o.reshape(n, l, D_HG), s_fin


def _conformer_conv(z, buf, w_dw, b_dw, ln_g, ln_b):
    a, gate = jnp.split(z, 2, axis=-1)
    u = a * jax.nn.sigmoid(gate)
    c, new_buf = _causal_dwconv(u, buf, w_dw, b_dw)
    return jax.nn.silu(_layer_norm(c, ln_g, ln_b)), new_buf


def _cross_attn(h, k, v, wq, wo):
    n, l, _ = h.shape
    q = (h @ wq).reshape(n, l, X_HEADS, X_HEAD_DIM)
    s = jnp.einsum('nlhd,nmhd->nhlm', q, k.astype(q.dtype)).astype(jnp.float32) * (X_HEAD_DIM ** -0.5)
    p = jax.nn.softmax(s, axis=-1).astype(v.dtype)
    o = jnp.einsum('nhlm,nmhd->nlhd', p, v).reshape(n, l, D_MODEL)
    return o @ wo


def _conv_ffn(h, buf, w_in, w_dw, b_dw, w_out):
    a, b = jnp.split(h @ w_in, 2, axis=-1)
    a, new_buf = _causal_dwconv(a, buf, w_dw, b_dw)
    return (jax.nn.gelu(a) * b) @ w_out, new_buf


def _trunk(x, mem_k, mem_v, s5_re, s5_im, hg_s, conf_buf, ffn_buf, w):
    new_re, new_im, new_hg, new_conf, new_ffn = [], [], [], [], []
    for li in range(DEPTH):
        g = w['norm_gains'][li]
        j = li // 2
        h = _rms_norm(x, g[NG_MIX_PRE])
        if li % 2 == 0:
            z = h @ w['w_ab_in'][j]
            u, q, fl, iv, og = jnp.split(z, [D_S5, D_S5 + D_HG, D_S5 + 2 * D_HG, D_S5 + 3 * D_HG], axis=-1)
            ya, hr, hi = _s5(u, s5_re[j], s5_im[j], w['s5_lambda_re'][j], w['s5_lambda_im'][j],
                             w['s5_b_re'][j], w['s5_b_im'][j], w['s5_c_re'][j], w['s5_c_im'][j],
                             w['s5_d'][j], w['s5_log_dt'][j], w['s5_w_glu'][j], w['s5_b_glu'][j])
            yb, hs = _hgrn2(q, fl, iv, og, hg_s[j], w['hg_lb'][j], w['hg_gnorm'][j])
            y = jnp.concatenate([ya.astype(h.dtype), yb.astype(h.dtype)], axis=-1) @ w['w_ab_out'][j]
            new_re.append(hr)
            new_im.append(hi)
            new_hg.append(hs)
        else:
            z = h @ w['w_conf_in'][j]
            c, cb = _conformer_conv(z, conf_buf[j], w['conf_dw'][j], w['conf_dw_b'][j],
                                    w['conf_ln_g'][j], w['conf_ln_b'][j])
            y = c @ w['w_conf_out'][j]
            new_conf.append(cb)
        x = x + _rms_norm(y, g[NG_MIX_POST]).astype(x.dtype)
        a = _cross_attn(_rms_norm(x, g[NG_X_PRE]), mem_k[li], mem_v[li], w['w_xq'][li], w['w_xo'][li])
        x = x + _rms_norm(a, g[NG_X_POST]).astype(x.dtype)
        f, fb = _conv_ffn(_rms_norm(x, g[NG_FFN_PRE]), ffn_buf[li], w['w_ffn_in'][li],
                          w['ffn_dw'][li], w['ffn_dw_b'][li], w['w_ffn_out'][li])
        x = x + _rms_norm(f, g[NG_FFN_POST]).astype(x.dtype)
        new_ffn.append(fb)
    return (x, jnp.stack(new_re), jnp.stack(new_im), jnp.stack(new_hg),
            jnp.stack(new_conf), jnp.stack(new_ffn))


def setup_inputs(seed: int = 0) -> dict:
    key = jax.random.key(seed)
    ks = iter(jax.random.split(key, 64))
    f32 = jnp.float32

    def nrm(shape, scale):
        return scale * jax.random.normal(next(ks), shape, f32)

    n_idx = jnp.arange(S5_STATE, dtype=f32)
    return {
        'x_prompt': nrm((BATCH, SEQ, D_MODEL), 1.0),
        'x_sample': nrm((DEC_BATCH, DEC_SEQ, D_MODEL), 1.0),
        'cache_mem_k': nrm((DEPTH, DEC_BATCH, N_MEM, X_HEADS, X_HEAD_DIM), 1.0),
        'cache_mem_v': nrm((DEPTH, DEC_BATCH, N_MEM, X_HEADS, X_HEAD_DIM), 1.0),
        'state_s5_re': nrm((N_AB, DEC_BATCH, S5_GROUPS, S5_STATE), 0.1),
        'state_s5_im': nrm((N_AB, DEC_BATCH, S5_GROUPS, S5_STATE), 0.1),
        'state_hgrn': nrm((N_AB, DEC_BATCH, HG_HEADS, HG_DK, HG_DV), 0.5),
        'state_conf': nrm((N_C, DEC_BATCH, CONF_W - 1, D_CONF), 0.5),
        'state_ffn': nrm((DEPTH, DEC_BATCH, FFN_W - 1, D_FF), 1.0),
        'mem_prompt': nrm((BATCH, N_MEM, D_MODEL), 1.0),
        'norm_gains': 1.0 + nrm((DEPTH, N_NORMS, D_MODEL), 0.05),
        'w_ab_in': nrm((N_AB, D_MODEL, D_AB_IN), D_MODEL ** -0.5),
        'w_ab_out': nrm((N_AB, D_S5 + D_HG, D_MODEL), (D_S5 + D_HG) ** -0.5),
        's5_lambda_re': -0.5 + nrm((N_AB, S5_GROUPS, S5_STATE), 0.01),
        's5_lambda_im': math.pi * n_idx + nrm((N_AB, S5_GROUPS, S5_STATE), 0.01),
        's5_b_re': nrm((N_AB, S5_GROUPS, S5_STATE, S5_GROUP), (2 * S5_GROUP) ** -0.5),
        's5_b_im': nrm((N_AB, S5_GROUPS, S5_STATE, S5_GROUP), (2 * S5_GROUP) ** -0.5),
        's5_c_re': nrm((N_AB, S5_GROUPS, S5_GROUP, S5_STATE), S5_STATE ** -0.5),
        's5_c_im': nrm((N_AB, S5_GROUPS, S5_GROUP, S5_STATE), S5_STATE ** -0.5),
        's5_d': nrm((N_AB, D_S5), 1.0),
        's5_log_dt': jax.random.uniform(next(ks), (N_AB, S5_GROUPS), f32, math.log(1e-3), math.log(1e-1)),
        's5_w_glu': nrm((N_AB, D_S5, D_S5), D_S5 ** -0.5),
        's5_b_glu': nrm((N_AB, D_S5), 0.01),
        'hg_lb_logits': nrm((N_AB, D_HG), 0.1),
        'hg_gnorm': 1.0 + nrm((N_AB, D_HG), 0.05),
        'w_conf_in': nrm((N_C, D_MODEL, 2 * D_CONF), D_MODEL ** -0.5),
        'conf_dw': nrm((N_C, CONF_W, D_CONF), CONF_W ** -0.5),
        'conf_dw_b': nrm((N_C, D_CONF), 0.01),
        'conf_ln_g': 1.0 + nrm((N_C, D_CONF), 0.05),
        'conf_ln_b': nrm((N_C, D_CONF), 0.01),
        'w_conf_out': nrm((N_C, D_CONF, D_MODEL), D_CONF ** -0.5),
        'w_xq': nrm((DEPTH, D_MODEL, D_MODEL), D_MODEL ** -0.5),
        'w_xk': nrm((DEPTH, D_MODEL, D_MODEL), D_MODEL ** -0.5),
        'w_xv': nrm((DEPTH, D_MODEL, D_MODEL), D_MODEL ** -0.5),
        'w_xo': nrm((DEPTH, D_MODEL, D_MODEL), D_MODEL ** -0.5),
        'w_ffn_in': nrm((DEPTH, D_MODEL, 2 * D_FF), D_MODEL ** -0.5),
        'ffn_dw': nrm((DEPTH, FFN_W, D_FF), FFN_W ** -0.5),
        'ffn_dw_b': nrm((DEPTH, D_FF), 0.01),
        'w_ffn_out': nrm((DEPTH, D_FF, D_MODEL), D_FF ** -0.5),
    }


def reference(x_prompt, x_sample, cache_mem_k, cache_mem_v, state_s5_re, state_s5_im, state_hgrn,
              state_conf, state_ffn, mem_prompt, norm_gains, w_ab_in, w_ab_out, s5_lambda_re,
              s5_lambda_im, s5_b_re, s5_b_im, s5_c_re, s5_c_im, s5_d, s5_log_dt, s5_w_glu, s5_b_glu,
              hg_lb_logits, hg_gnorm, w_conf_in, conf_dw, conf_dw_b, conf_ln_g, conf_ln_b, w_conf_out,
              w_xq, w_xk, w_xv, w_xo, w_ffn_in, ffn_dw, ffn_dw_b, w_ffn_out):
    f32 = jnp.float32
    sm = jax.nn.softmax(hg_lb_logits.astype(f32), axis=0)
    hg_lb = jnp.cumsum(sm, axis=0) - sm[0]
    w = dict(norm_gains=norm_gains, w_ab_in=w_ab_in, w_ab_out=w_ab_out,
             s5_lambda_re=s5_lambda_re, s5_lambda_im=s5_lambda_im, s5_b_re=s5_b_re, s5_b_im=s5_b_im,
             s5_c_re=s5_c_re, s5_c_im=s5_c_im, s5_d=s5_d, s5_log_dt=s5_log_dt,
             s5_w_glu=s5_w_glu, s5_b_glu=s5_b_glu, hg_lb=hg_lb, hg_gnorm=hg_gnorm,
             w_conf_in=w_conf_in, conf_dw=conf_dw, conf_dw_b=conf_dw_b, conf_ln_g=conf_ln_g,
             conf_ln_b=conf_ln_b, w_conf_out=w_conf_out, w_xq=w_xq, w_xo=w_xo,
             w_ffn_in=w_ffn_in, ffn_dw=ffn_dw, ffn_dw_b=ffn_dw_b, w_ffn_out=w_ffn_out)

    nb, n_mem, _ = mem_prompt.shape
    mk, mv = [], []
    for li in range(DEPTH):
        m = _rms_norm(mem_prompt, norm_gains[li, NG_MEM])
        mk.append((m @ w_xk[li]).reshape(nb, n_mem, X_HEADS, X_HEAD_DIM))
        mv.append((m @ w_xv[li]).reshape(nb, n_mem, X_HEADS, X_HEAD_DIM))
    p_mem_k = jnp.stack(mk)
    p_mem_v = jnp.stack(mv)

    nbp = x_prompt.shape[0]
    z_re = jnp.zeros((N_AB, nbp, S5_GROUPS, S5_STATE), f32)
    z_hg = jnp.zeros((N_AB, nbp, HG_HEADS, HG_DK, HG_DV), f32)
    z_conf = jnp.zeros((N_C, nbp, CONF_W - 1, D_CONF), x_prompt.dtype)
    z_ffn = jnp.zeros((DEPTH, nbp, FFN_W - 1, D_FF), x_prompt.dtype)
    y_prompt, p_re, p_im, p_hg, p_conf, p_ffn = _trunk(
        x_prompt, p_mem_k, p_mem_v, z_re, z_re, z_hg, z_conf, z_ffn, w)

    y_sample, s_re, s_im, s_hg, s_conf, s_ffn = _trunk(
        x_sample, cache_mem_k, cache_mem_v, state_s5_re, state_s5_im, state_hgrn,
        state_conf, state_ffn, w)

    return (y_prompt, y_sample, p_re, p_im, p_hg, p_conf, p_ffn, p_mem_k, p_mem_v,
            s_re, s_im, s_hg, s_conf, s_ffn)
```

```python
import math
import bisect
import numpy as np
import concourse.bass as bass
import concourse.mybir as mybir
from concourse.bass_utils import run_bass_kernel_spmd

F32 = mybir.dt.float32
BF16 = mybir.dt.bfloat16
ALU = mybir.AluOpType
AF = mybir.ActivationFunctionType
AX = mybir.AxisListType

NCORE = 8
D = 1024
KC = 8
DEPTH = 4
TP = 2048
STK = 1024
NSUP = 2
NS = 16
W = STK + NS
DFF = 2816
FC = 22
NMEM = 256
EPS = 1e-6
NDMASEM = 16
NSTAGE = 4
ENGS = ['pe', 'dve', 'act', 'pool', 'sp']
STOP_AFTER = None
DBG = set()
NRUN = 8
USE_BARRIER = False
NF_ = 32200
NB_ = 48000


class Seg:
    __slots__ = ('lw', 'lwgrp', 'rd', 'prd', 'plw')

    def __init__(self, o=None):
        cpd = lambda d: {k: (list(v) if isinstance(v, list) else v) for k, v in d.items()}
        if o is None:
            self.lw = {}
            self.lwgrp = None
            self.rd = {}
            self.prd = {}
            self.plw = {}
        else:
            self.lw = cpd(o.lw)
            self.lwgrp = o.lwgrp
            self.rd = cpd(o.rd)
            self.prd = cpd(o.prd)
            self.plw = cpd(o.plw)


class Space:
    def __init__(self, n):
        self.n = n
        self.bounds = [0]
        self.st = {0: Seg()}

    def split(self, x):
        if x <= 0 or x >= self.n:
            return
        i = bisect.bisect_right(self.bounds, x) - 1
        a = self.bounds[i]
        if a == x:
            return
        self.bounds.insert(i + 1, x)
        self.st[x] = Seg(self.st[a])

    def segs(self, a, b):
        i = bisect.bisect_left(self.bounds, a)
        out = []
        while i < len(self.bounds) and self.bounds[i] < b:
            out.append(self.st[self.bounds[i]])
            i += 1
        return out


class Buf:
    def __init__(self, ap, space=None, a=0, b=1):
        self.ap = ap
        self.space = space if space is not None else Space(1)
        self.a = a
        self.b = b
        self.excl = False

    def segs(self):
        return self.space.segs(self.a, self.b)

    def __getitem__(self, k):
        return self.ap[k]


def _add(d, me):
    e, i = me
    if e == 'sp':
        d.setdefault('sp', []).append(i)
    else:
        d[e] = max(d.get(e, -1), i)


def _items(d):
    for e, v in d.items():
        if e == 'sp':
            for i in v:
                yield ('sp', i)
        else:
            yield (e, v)


class Prog:
    def __init__(self):
        self.ops = {e: [] for e in ENGS}

    def op(self, eng, fn, rd=(), wr=(), grp=None):
        idx = len(self.ops[eng])
        me = (eng, idx)
        deps = set()
        wr = list(wr) + [b for b in rd if b.excl and b not in wr]
        rd = [b for b in rd if not b.excl]
        rds = [g for b in rd for g in b.segs()]
        wrs = [g for b in wr for g in b.segs()]
        for b in rds:
            deps.update(_items(b.lw))
        for b in wrs:
            if grp is not None and b.lwgrp == grp:
                for it in _items(b.rd):
                    _add(b.prd, it)
            else:
                b.prd = b.rd
                b.plw = b.lw
                b.lw = {}
                b.lwgrp = grp
            b.rd = {}
            deps.update(_items(b.prd))
            deps.update(_items(b.plw))
        if eng == 'pe':
            deps = {d for d in deps if d[0] != 'pe'}
        deps.discard(me)
        for b in rds:
            _add(b.rd, me)
        for b in wrs:
            _add(b.lw, me)
        self.ops[eng].append((fn, deps))

    def barrier(self):
        if not USE_BARRIER:
            return
        last = []
        for e in ENGS:
            i = len(self.ops[e]) - 1
            while i >= 0 and self.ops[e][i][0] is None:
                i -= 1
            if i >= 0:
                last.append((e, i))
        for e in ENGS:
            if e == 'sp':
                continue
            deps = {d for d in last if d[0] != e}
            deps.update(('sp', i) for i in range(max(0, len(self.ops['sp']) - NDMASEM), len(self.ops['sp'])))
            self.ops[e].append((None, deps))

    def emit(self, nc, block, sems, dsems):
        flagged = {e: set() for e in ENGS}
        for e in ENGS:
            for fn, deps in self.ops[e]:
                for d in deps:
                    flagged[d[0]].add(d[1])
        cnt = {e: {} for e in ENGS}
        for e in ENGS:
            c = 0
            for i in range(len(self.ops[e])):
                if i in flagged[e]:
                    c += 1
                cnt[e][i] = c
        K = NDMASEM

        def run(ename, eng):
            known = {}

            def wait(key, semh, val):
                if known.get(key, 0) < val:
                    eng.wait_ge(semh, val)
                    known[key] = val
            for i, (fn, deps) in enumerate(self.ops[ename]):
                for d in sorted(deps):
                    if d[0] == 'sp':
                        wait(('d', d[1] % K), dsems[d[1] % K], 16 * (d[1] // K + 1))
                    else:
                        wait(d[0], sems[d[0]], cnt[d[0]][d[1]])
                if ename == 'sp' and i >= K:
                    wait(('d', i % K), dsems[i % K], 16 * (i // K))
                if fn is None:
                    if i in flagged[ename]:
                        eng.sem_inc(sems[ename], 1)
                    continue
                ins = fn(eng)
                if ename == 'sp':
                    ins.then_inc(dsems[i % K], 16)
                elif i in flagged[ename]:
                    ins.then_inc(sems[ename], 1)
            if ename == 'sp':
                n = len(self.ops['sp'])
                for k in range(K):
                    uses = (n - k + K - 1) // K if n > k else 0
                    if uses:
                        wait(('d', k), dsems[k], 16 * uses)

        block.tensor(lambda e: run('pe', e))
        block.vector(lambda e: run('dve', e))
        block.scalar(lambda e: run('act', e))
        block.gpsimd(lambda e: run('pool', e))
        block.sync(lambda e: run('sp', e))


class Arena:
    def __init__(self, ap, n):
        self.ap = ap
        self.n = n
        self.off = 0
        self.peak = 0
        self.space = Space(n)

    def alloc(self, *shape):
        sz = int(np.prod(shape))
        sz_al = (sz + 15) // 16 * 16
        assert self.off + sz_al <= self.n, f"arena overflow {self.off}+{sz_al}>{self.n}"
        v = self.ap[:, self.off:self.off + sz]
        a0 = self.off
        self.space.split(a0)
        self.space.split(a0 + sz_al)
        self.off += sz_al
        self.peak = max(self.peak, self.off)
        if len(shape) == 2:
            v = v.rearrange("p (a b) -> p a b", a=shape[0])
        elif len(shape) == 3:
            v = v.rearrange("p (a b c) -> p a b c", a=shape[0], b=shape[1])
        elif len(shape) == 4:
            v = v.rearrange("p (a b c d) -> p a b c d", a=shape[0], b=shape[1], c=shape[2])
        return Buf(v, self.space, a0, a0 + sz_al)


def build():
    nc = bass.Bass("TRN2", target_bir_lowering=False, dynamic_dma_scratch_size=2048)
    P = Prog()

    def din(name, shape):
        return nc.dram_tensor(name, list(shape), F32, kind="ExternalInput").ap()

    def dout(name, shape):
        return nc.dram_tensor(name, list(shape), F32, kind="ExternalOutput").ap()

    xT = din("xT", [D, TP]); xsT = din("xsT", [D, NS]); memT = din("memT", [D, NMEM])
    kcT = din("kcT", [DEPTH, NS, 4, 256, 256]); vc = din("vc", [DEPTH, NS, 256, D])
    s5r = din("s5r", [2, 128, 16, NS]); s5i = din("s5i", [2, 128, 16, NS])
    hgs = din("hgs", [2, NS, 4, 128, 128])
    cfs = din("cfs", [2, 128, 8, NS, 30]); ffs = din("ffs", [DEPTH, 128, FC, NS, 2])
    gains = din("gains", [128, DEPTH, 7, 8])
    w_ab_in = din("w_ab_in", [2, D, 2560]); w_ab_out = din("w_ab_out", [2, D, D])
    w_glu = din("w_glu", [2, 512, 512])
    w_conf_in = din("w_conf_in", [2, D, 2048]); w_conf_out = din("w_conf_out", [2, D, D])
    w_xq = din("w_xq", [DEPTH, D, D]); w_xk = din("w_xk", [DEPTH, D, D])
    w_xv = din("w_xv", [DEPTH, D, D]); w_xo = din("w_xo", [DEPTH, D, D])
    w_ffn_in = din("w_ffn_in", [DEPTH, D, 2 * DFF]); w_ffn_out = din("w_ffn_out", [DEPTH, DFF, D])
    lamr_p = din("lamr_p", [2, 128, 16]); lami_p = din("lami_p", [2, 128, 16]); ldt_p = din("ldt_p", [2, 128, 16])
    brN = din("brN", [2, 128, 16, 128]); biN = din("biN", [2, 128, 16, 128])
    crN = din("crN", [2, 128, 16, 128]); ciN = din("ciN", [2, 128, 16, 128])
    s5d = din("s5d", [128, 2, 4]); bglu = din("bglu", [128, 2, 4])
    lbl_p = din("lbl_p", [128, 2, 4]); lbl_b = din("lbl_b", [2, 512]); gnp = din("gnp", [128, 2, 4])
    cdw = din("cdw", [128, 2, 8, 31]); cdb = din("cdb", [128, 2, 8]); clg = din("clg", [128, 2, 8]); clb = din("clb", [128, 2, 8])
    fdw = din("fdw", [128, DEPTH, FC, 3]); fdb = din("fdb", [128, DEPTH, FC])
    c_ident = din("c_ident", [128, 128]); c_mask = din("c_mask", [128, 128]); c_cmask = din("c_cmask", [128, 4])

    yT = dout("yT", [D, TP]); ysT = dout("ysT", [D, NS])
    o_ps5r = dout("o_ps5r", [2, 128, 16]); o_ps5i = dout("o_ps5i", [2, 128, 16])
    o_phg = dout("o_phg", [2, 4, 128, 128])
    o_pconf = dout("o_pconf", [2, 128, 8, 30]); o_pffn = dout("o_pffn", [DEPTH, 128, FC, 2])
    o_pmk = dout("o_pmk", [DEPTH, NMEM, D]); o_pmv = dout("o_pmv", [DEPTH, NMEM, D])
    o_ss5r = dout("o_ss5r", [2, 128, 16, NS]); o_ss5i = dout("o_ss5i", [2, 128, 16, NS])
    o_shg = dout("o_shg", [2, NS, 4, 128, 128])
    o_sconf = dout("o_sconf", [2, 128, 8, NS, 30]); o_sffn = dout("o_sffn", [DEPTH, 128, FC, NS, 2])

    NF = NF_
    NB = NB_
    ctx = []

    def enter(c):
        ctx.append(c)
        return c.__enter__()

    af_t = enter(nc.sbuf_tensor("af", [128, NF], F32))
    ab_t = enter(nc.sbuf_tensor("ab", [128, NB], BF16))
    FA = Arena(af_t, NF)
    BA = Arena(ab_t, NB)
    psb = [Buf(enter(nc.psum_tensor(f"ps{i}", [128, 512], F32))[:, :]) for i in range(7)]
    psbf = Buf(enter(nc.psum_tensor("psbf", [128, 1024], BF16))[:, :])
    for b_ in psb + [psbf]:
        b_.excl = True
    sems = {e: enter(nc.semaphore("s_" + e)) for e in ENGS if e != 'sp'}
    dsems = [enter(nc.semaphore(f"d{i}")) for i in range(NDMASEM)]
    pscnt = [0]

    psrot = [4]

    def psum():
        b = psb[pscnt[0] % psrot[0]]
        pscnt[0] += 1
        return b

    def psfix(i):
        return psb[4 + i]

    def dma(out, in_, rd=(), wr=(), grp=None, **kw):
        P.op('sp', lambda e, o=out, i=in_: e.dma_start(out=o, in_=i, **kw), rd, wr, grp)

    def mm(out, lhsT, rhs, start, stop, rd, wr):
        P.op('pe', lambda e: e.matmul(out, lhsT=lhsT, rhs=rhs, start=start, stop=stop), rd, wr)

    def act(out, in_, func, rd, wr, bias=None, scale=None, grp=None, eng='act'):
        kw = {}
        if bias is not None:
            kw['bias'] = bias
        if scale is not None:
            kw['scale'] = scale
        P.op('act', lambda e: e.activation(out=out, in_=in_, func=func, **kw), rd, wr, grp)

    def tt(eng, out, in0, in1, op, rd, wr, grp=None):
        P.op(eng, lambda e: e.tensor_tensor(out=out, in0=in0, in1=in1, op=op), rd, wr, grp)

    def ts(eng, out, in0, s1, op0, rd, wr, s2=None, op1=None, grp=None):
        if op1 is None:
            P.op(eng, lambda e: e.tensor_scalar(out=out, in0=in0, scalar1=s1, scalar2=None, op0=op0), rd, wr, grp)
        else:
            P.op(eng, lambda e: e.tensor_scalar(out=out, in0=in0, scalar1=s1, scalar2=s2, op0=op0, op1=op1), rd, wr, grp)

    def stt(out, in0, scalar, in1, op0, op1, rd, wr, grp=None):
        P.op('dve', lambda e: e.scalar_tensor_tensor(out=out, in0=in0, scalar=scalar, in1=in1, op0=op0, op1=op1), rd, wr, grp)

    def cp(eng, out, in_, rd, wr, grp=None):
        if eng == 'act':
            P.op('act', lambda e: e.activation(out=out, in_=in_, func=AF.Copy), rd, wr, grp)
        else:
            P.op(eng, lambda e: e.tensor_copy(out=out, in_=in_), rd, wr, grp)

    def memset(eng, buf, ap, val):
        P.op(eng, lambda e: e.memset(ap, val), (), [buf])

    def bc(ap2d, n):
        return ap2d.unsqueeze(1).to_broadcast([128, n, ap2d.shape[1]])

    x = FA.alloc(KC, W)
    gn_all = FA.alloc(DEPTH, 7, 8)
    ident = FA.alloc(128)
    epsb = FA.alloc(1)
    t_s5d = FA.alloc(2, 4); t_bglu = FA.alloc(2, 4); t_lbl = FA.alloc(2, 4); t_gnp = FA.alloc(2, 4)
    t_cdw = FA.alloc(2, 8, 31); t_cdb = FA.alloc(2, 8); t_clg = FA.alloc(2, 8); t_clb = FA.alloc(2, 8)
    t_fdw = FA.alloc(DEPTH, FC, 3); t_fdb = FA.alloc(DEPTH, FC)
    lb_p = FA.alloc(2, 4); omlb_p = FA.alloc(2, 4)
    hg_S = [[FA.alloc(128) for _h in range(4)] for _ in range(2)]
    s5car = [(FA.alloc(16), FA.alloc(16)) for _ in range(2)]
    ffn_hist = [FA.alloc(FC, 2) for _ in range(DEPTH)]
    mask16 = FA.alloc(128)
    cmask = FA.alloc(4)
    ones_bf = BA.alloc(128)
    ident_bf = BA.alloc(128)
    conf_hist = [BA.alloc(8, 30) for _ in range(2)]
    mhat = BA.alloc(KC, NMEM)
    stage = [FA.alloc(1024) for _ in range(NSTAGE)]
    stcnt = [0]

    for (t, src) in [(gn_all, gains), (t_s5d, s5d), (t_bglu, bglu), (t_lbl, lbl_p), (t_gnp, gnp), (t_cdw, cdw),
                     (t_cdb, cdb), (t_clg, clg), (t_clb, clb), (t_fdw, fdw), (t_fdb, fdb)]:
        dma(t.ap, src, wr=[t])
    dma(ident.ap, c_ident, wr=[ident])
    dma(mask16.ap, c_mask, wr=[mask16])
    dma(cmask.ap, c_cmask, wr=[cmask])
    cp('dve', ident_bf.ap, ident.ap, [ident], [ident_bf])
    memset('dve', ones_bf, ones_bf.ap, 1.0)
    memset('dve', epsb, epsb.ap, EPS)
    memset('dve', lb_p, lb_p.ap[:, 0, :], 0.0)
    tt('dve', lb_p.ap[:, 1, :], t_lbl.ap[:, 1, :], t_lbl.ap[:, 0, :], ALU.subtract, [t_lbl], [lb_p])
    act(lb_p.ap[:, 1, :], lb_p.ap[:, 1, :], AF.Sigmoid, [lb_p], [lb_p])
    ts('dve', omlb_p.ap, lb_p.ap, -1.0, ALU.mult, [lb_p], [omlb_p], s2=1.0, op1=ALU.add)
    for j in range(2):
        for _h in range(4):
            memset('dve', hg_S[j][_h], hg_S[j][_h].ap, 0.0)
        memset('dve', s5car[j][0], s5car[j][0].ap, 0.0)
        memset('dve', s5car[j][1], s5car[j][1].ap, 0.0)
        memset('pool', conf_hist[j], conf_hist[j].ap, 0.0)
    for li in range(DEPTH):
        memset('dve', ffn_hist[li], ffn_hist[li].ap, 0.0)
    fmark = FA.off
    bmark = BA.off

    cast_rr = [0]

    def load_w(dst, dview, src, kc_n, ncols, gain=None):
        if ncols <= 128:
            st = stage[stcnt[0] % NSTAGE]
            stcnt[0] += 1
            sv = st.ap[:, 0:kc_n * ncols].rearrange("p (a b) -> p a b", a=kc_n)
            dma(sv, src.rearrange("(k p) c -> p k c", p=128), wr=[st])
            eng = ['act', 'dve'][cast_rr[0] % 2]
            cast_rr[0] += 1
            cp(eng, dview, sv, [st], [dst], grp='ldw')
            return
        for k in range(kc_n):
            for c0 in range(0, ncols, 1024):
                cw = min(1024, ncols - c0)
                st = stage[stcnt[0] % NSTAGE]
                stcnt[0] += 1
                dma(st.ap[:, 0:cw], src[k * 128:(k + 1) * 128, c0:c0 + cw], wr=[st])
                eng = ['act', 'dve'][cast_rr[0] % 2]
                cast_rr[0] += 1
                cp(eng, dview[:, k, c0:c0 + cw], st.ap[:, 0:cw], [st], [dst], grp='ldw')

    def subtiles(s):
        r = [(0, 512, False), (512, 512, False)]
        if s == 1:
            r.append((STK, NS, True))
        return r

    def rstd_of(src_fn, nch, c0, n, inv_dim, rdbufs, src_all=None):
        sq = BA.alloc(nch, 512)
        if src_all is not None:
            act(sq.ap[:, :, 0:n], src_all, AF.Square, rdbufs, [sq])
        else:
            for c in range(nch):
                act(sq.ap[:, c, 0:n], src_fn(c), AF.Square, rdbufs, [sq], grp='sq')
        ps = psum()
        for c in range(nch):
            mm(ps.ap[:, 0:n], ones_bf.ap, sq.ap[:, c, 0:n], c == 0, c == nch - 1, [ones_bf, sq], [ps])
        rs = FA.alloc(512)
        act(rs.ap[:, 0:n], ps.ap[:, 0:n], AF.Ln, [ps, epsb], [rs], bias=epsb.ap[:, 0:1], scale=inv_dim)
        act(rs.ap[:, 0:n], rs.ap[:, 0:n], AF.Exp, [rs], [rs], scale=-0.5)
        return rs

    def prenorm(s, hb, gain):
        for (c0, n, smp) in subtiles(s):
            fm, bm = FA.off, BA.off
            rs = rstd_of(lambda c: x.ap[:, c, c0:c0 + n], KC, c0, n, 1.0 / D, [x], src_all=x.ap[:, :, c0:c0 + n])
            for c in range(KC):
                stt(hb.ap[:, c, c0:c0 + n], x.ap[:, c, c0:c0 + n], gain[:, c:c + 1], rs.ap[:, 0:n], ALU.mult, ALU.mult,
                    [x, rs, gn_all], [hb], grp=('pn', c0))
            FA.off, BA.off = fm, bm

    def postnorm(s, yb, li, ng):
        for (c0, n, smp) in subtiles(s):
            fm, bm = FA.off, BA.off
            rs = rstd_of(lambda c: yb.ap[:, c, c0:c0 + n], KC, c0, n, 1.0 / D, [yb], src_all=yb.ap[:, :, c0:c0 + n])
            for c in range(KC):
                stt(yb.ap[:, c, c0:c0 + n], yb.ap[:, c, c0:c0 + n], gn_all.ap[:, li, ng, c:c + 1], rs.ap[:, 0:n],
                    ALU.mult, ALU.mult, [yb, gn_all, rs], [yb], grp=('po', c0))
            tt('dve', x.ap[:, :, c0:c0 + n], x.ap[:, :, c0:c0 + n], yb.ap[:, :, c0:c0 + n], ALU.add, [x, yb], [x])
            FA.off, BA.off = fm, bm

    def linear(wb, wview, col0, mchunks, kch, rhs_fn, rdbufs, n, evac):
        for m in range(mchunks):
            ps = psum()
            for k in range(kch):
                mm(ps.ap[:, 0:n], wview[:, k, col0 + m * 128: col0 + (m + 1) * 128], rhs_fn(k), k == 0, k == kch - 1,
                   [wb] + rdbufs, [ps])
            evac(m, ps)

    def gelu(out, in_, rd, wr, grp=None):
        act(out, in_, AF.Gelu_apprx_tanh, rd, wr, grp=grp)

    nsub = [0]

    def done():
        nsub[0] += 1
        return STOP_AFTER is not None and nsub[0] >= STOP_AFTER

    def reset():
        P.barrier()
        FA.off, BA.off = fmark, bmark

    def mem_prep():
        mt = FA.alloc(KC, NMEM)
        dma(mt.ap, memT.rearrange("(c p) m -> p c m", p=128), wr=[mt])
        rs = rstd_of(lambda c: mt.ap[:, c, :], KC, 0, NMEM, 1.0 / D, [mt])
        tt('dve', mhat.ap, mt.ap, bc(rs.ap[:, 0:NMEM], KC), ALU.mult, [mt, rs], [mhat])
        reset()

    def xattn(s, li):
        psrot[0] = 7
        gq = gn_all.ap[:, li, 2, :]
        gm = gn_all.ap[:, li, 6, :]
        kfm = BA.alloc(KC, NMEM)
        vtm = BA.alloc(2, D)
        mg = BA.alloc(KC, NMEM)
        for c in range(KC):
            ts('dve', mg.ap[:, c, :], mhat.ap[:, c, :], gm[:, c:c + 1], ALU.mult, [mhat, gn_all], [mg], grp='mg')
        wk = BA.alloc(KC, D)
        load_w(wk, wk.ap, w_xk[li], KC, D)
        otm = FA.alloc(2, D)
        for mb in range(2):
            for half in range(2):
                ps = psum()
                for k in range(KC):
                    mm(ps.ap, mg.ap[:, k, mb * 128:(mb + 1) * 128], wk.ap[:, k, half * 512:(half + 1) * 512], k == 0, k == KC - 1, [mg, wk], [ps])
                cp('act', otm.ap[:, mb, half * 512:(half + 1) * 512], ps.ap, [ps], [otm], grp='otm')
        dma(o_pmk[li].rearrange("(b p) d -> p b d", p=128), otm.ap, rd=[otm])
        linear(wk, wk.ap, 0, KC, KC, lambda k: mg.ap[:, k, :], [mg], NMEM,
               lambda m, ps: cp('dve', kfm.ap[:, m, :], ps.ap[:, 0:NMEM], [ps], [kfm], grp='kfm'))
        wv = BA.alloc(KC, D)
        load_w(wv, wv.ap, w_xv[li], KC, D)
        otv = FA.alloc(2, D)
        for mb in range(2):
            for half in range(2):
                ps = psum()
                for k in range(KC):
                    mm(ps.ap, mg.ap[:, k, mb * 128:(mb + 1) * 128], wv.ap[:, k, half * 512:(half + 1) * 512], k == 0, k == KC - 1, [mg, wv], [ps])
                cp('act', otv.ap[:, mb, half * 512:(half + 1) * 512], ps.ap, [ps], [otv], grp='otv')
                cp('dve', vtm.ap[:, mb, half * 512:(half + 1) * 512], ps.ap, [ps], [vtm], grp='vtm')
        dma(o_pmv[li].rearrange("(b p) d -> p b d", p=128), otv.ap, rd=[otv])
        P.barrier()
        FA.off = fmark
        BA.off = bmark + KC * NMEM + 2 * D
        hb = BA.alloc(KC, W)
        wq = BA.alloc(KC, D)
        load_w(wq, wq.ap, w_xq[li], KC, D)
        wo = BA.alloc(KC, D)
        load_w(wo, wo.ap, w_xo[li], KC, D)
        prenorm(s, hb, gq)
        usets = [dict(mx=FA.alloc(4), pe=FA.alloc(4, NMEM), sm=FA.alloc(4), pb=BA.alloc(4, NMEM), ptb=BA.alloc(8, 128)) for _ in range(2)]
        ucnt = 0
        yb = FA.alloc(KC, W)
        scl = 1.0 / 16.0
        for (c0, n, smp) in subtiles(s):
            fm, bm = FA.off, BA.off
            qb = BA.alloc(KC, 512)
            ob = BA.alloc(KC, 512)
            if not smp:
                linear(wq, wq.ap, 0, KC, KC, lambda k: hb.ap[:, k, c0:c0 + n], [hb], n,
                       lambda m, ps: cp('act', qb.ap[:, m, 0:n], ps.ap[:, 0:n], [ps], [qb], grp='qb'))
                def scores(blk):
                    pss_ = [psum(), psum()]
                    for h in range(4):
                        ps = pss_[h // 2]
                        off = (h % 2) * NMEM
                        for dc in range(2):
                            mm(ps.ap[:, off:off + NMEM], qb.ap[:, 2 * h + dc, blk * 128:(blk + 1) * 128], kfm.ap[:, 2 * h + dc, :], dc == 0, dc == 1, [qb, kfm], [ps])
                    return pss_
                nblk = n // 128
                sc = scores(0)
                for blk in range(nblk):
                    nxt = scores(blk + 1) if blk + 1 < nblk else None
                    us = usets[ucnt % 2]
                    ucnt += 1
                    mx4, sm4, pe4, pb4, ptb4 = us['mx'], us['sm'], us['pe'], us['pb'], us['ptb']
                    for q in range(2):
                        P.op('dve', lambda e, o=mx4.ap[:, 2 * q:2 * q + 2], i=sc[q].ap.rearrange("p (a b) -> p a b", a=2): e.reduce_max(out=o, in_=i, axis=AX.X), [sc[q]], [mx4], 'mx')
                    ts('dve', mx4.ap, mx4.ap, -scl, ALU.mult, [mx4], [mx4])
                    for h in range(4):
                        P.op('act', lambda e, o=pe4.ap[:, h, :], i=sc[h // 2].ap[:, (h % 2) * NMEM:(h % 2 + 1) * NMEM], b=mx4.ap[:, h:h + 1], a=sm4.ap[:, h:h + 1]:
                             e.activation(out=o, in_=i, func=AF.Exp, bias=b, scale=scl, accum_out=a), [sc[h // 2], mx4], [pe4, sm4], 'ex')
                    P.op('dve', lambda e, o=sm4.ap, i=sm4.ap: e.reciprocal(out=o, in_=i), [sm4], [sm4])
                    tt('dve', pb4.ap, pe4.ap, sm4.ap.unsqueeze(2).to_broadcast([128, 4, NMEM]), ALU.mult, [pe4, sm4], [pb4])
                    for h in range(4):
                        for mb in range(2):
                            P.op('pe', lambda e, o=psbf.ap[:, (h * 2 + mb) * 128:(h * 2 + mb + 1) * 128], i=pb4.ap[:, h, mb * 128:(mb + 1) * 128]:
                                 e.transpose(out=o, in_=i, identity=ident_bf.ap), [pb4, ident_bf], [psbf])
                    cp('act', ptb4.ap, psbf.ap.rearrange("p (a b) -> p a b", a=8), [psbf], [ptb4])
                    for q in range(2):
                        ps2 = psum()
                        for r in range(4):
                            hd = 4 * q + r
                            h = hd // 2
                            for mb in range(2):
                                mm(ps2.ap[:, r * 128:(r + 1) * 128], vtm.ap[:, mb, hd * 128:(hd + 1) * 128], ptb4.ap[:, h * 2 + mb, :], mb == 0, mb == 1, [vtm, ptb4], [ps2])
                        cp('dve', ob.ap[:, 4 * q:4 * q + 4, blk * 128:(blk + 1) * 128], ps2.ap.rearrange("p (a b) -> p a b", a=4), [ps2], [ob], grp='ob')
                    sc = nxt
            else:
                psrot[0] = 4
                qf = FA.alloc(KC, NS)
                linear(wq, wq.ap, 0, KC, KC, lambda k: hb.ap[:, k, c0:c0 + n], [hb], n,
                       lambda m, ps: cp('act', qf.ap[:, m, :], ps.ap[:, 0:n], [ps], [qf], grp='qf'))
                scT = FA.alloc(2, 64)
                kts = [FA.alloc(KC * NMEM) for _ in range(2)]
                vts = kts
                pss = psfix(0)
                for nn in range(NS):
                    kt = kts[nn % 2]
                    ktv = kt.ap.rearrange("p (a b) -> p a b", a=KC)
                    dma(ktv, kcT[li, nn].rearrange("h (c p) m -> p (h c) m", p=128), wr=[kt])
                    for h in range(4):
                        for mb in range(2):
                            col = mb * 64 + nn * 4 + h
                            for dc in range(2):
                                mm(pss.ap[:, col:col + 1], ktv[:, 2 * h + dc, mb * 128:(mb + 1) * 128], qf.ap[:, 2 * h + dc, nn:nn + 1], dc == 0, dc == 1, [kt, qf], [pss])
                cp('dve', scT.ap, pss.ap[:, 0:128].rearrange("p (a b) -> p a b", a=2), [pss], [scT])
                pst = psum()
                for mb in range(2):
                    P.op('pe', lambda e, o=pst.ap[0:64, mb * 128:(mb + 1) * 128], i=scT.ap[:, mb, :]:
                         e.transpose(out=o, in_=i, identity=ident.ap), [scT, ident], [pst])
                mx = FA.alloc(1)
                P.op('dve', lambda e, o=mx.ap[0:64, :], i=pst.ap[0:64, 0:NMEM]: e.reduce_max(out=o, in_=i, axis=AX.X), [pst], [mx])
                ts('dve', mx.ap[0:64, :], mx.ap[0:64, :], -scl, ALU.mult, [mx], [mx])
                pe_ = FA.alloc(NMEM)
                sm = FA.alloc(1)
                P.op('act', lambda e, o=pe_.ap[0:64, :], i=pst.ap[0:64, 0:NMEM], b=mx.ap[0:64, 0:1], a=sm.ap[0:64, 0:1]:
                     e.activation(out=o, in_=i, func=AF.Exp, bias=b, scale=scl, accum_out=a), [pst, mx], [pe_, sm])
                P.op('dve', lambda e, o=sm.ap[0:64, :], i=sm.ap[0:64, :]: e.reciprocal(out=o, in_=i), [sm], [sm])
                ts('dve', pe_.ap[0:64, :], pe_.ap[0:64, :], sm.ap[0:64, 0:1], ALU.mult, [pe_, sm], [pe_])
                pT = FA.alloc(2, 64)
                pst2 = psum()
                for mb in range(2):
                    P.op('pe', lambda e, o=pst2.ap[:, mb * 64:(mb + 1) * 64], i=pe_.ap[0:64, mb * 128:(mb + 1) * 128]:
                         e.transpose(out=o, in_=i, identity=ident.ap[0:64, 0:64]), [pe_, ident], [pst2])
                cp('dve', pT.ap, pst2.ap[:, 0:128].rearrange("p (a b) -> p a b", a=2), [pst2], [pT])
                pso = psfix(1)
                for nn in range(NS):
                    vt = vts[nn % 2]
                    vtv = vt.ap.rearrange("p (a b) -> p a b", a=2)
                    dma(vtv, vc[li, nn].rearrange("(b p) d -> p b d", p=128), wr=[vt])
                    for h in range(4):
                        for dc in range(2):
                            col = (2 * h + dc) * NS + nn
                            for mb in range(2):
                                mm(pso.ap[:, col:col + 1], vtv[:, mb, (2 * h + dc) * 128:(2 * h + dc + 1) * 128],
                                   pT.ap[:, mb, nn * 4 + h:nn * 4 + h + 1], mb == 0, mb == 1, [vt, pT], [pso])
                cp('dve', ob.ap[:, :, 0:NS], pso.ap[:, 0:KC * NS].rearrange("p (a b) -> p a b", a=KC), [pso], [ob])
            linear(wo, wo.ap, 0, KC, KC, lambda k: ob.ap[:, k, 0:n], [ob], n,
                   lambda m, ps: cp('act', yb.ap[:, m, c0:c0 + n], ps.ap[:, 0:n], [ps], [yb], grp=('yb', c0)))
            FA.off, BA.off = fm, bm
        postnorm(s, yb, li, 3)
        psrot[0] = 4
        reset()

    def ffn(s, li):
        psrot[0] = 7
        hb = BA.alloc(KC, W)
        gf = gn_all.ap[:, li, 4, :]
        yb = FA.alloc(KC, W)
        win = w_ffn_in[li]
        wout = w_ffn_out[li]
        HALF = FC // 2
        gb = BA.alloc(HALF, W)
        wabs = [BA.alloc(KC, 256) for _ in range(4)]
        wo = BA.alloc(HALF, D)
        asb = [FA.alloc(2 + 512) for _ in range(2)]
        tb = [FA.alloc(512) for _ in range(2)]
        sth = FA.alloc(FC, NS, 2)
        sout = FA.alloc(FC, NS, 2)
        pout = FA.alloc(FC, 2)
        if s == 1:
            dma(sth.ap, ffs[li], wr=[sth])
            cp('pool', sout.ap[:, :, :, 0], sth.ap[:, :, :, 1], [sth], [sout])
        hist = ffn_hist[li]
        it = 0

        def ldchunk(c):
            wab = wabs[c % 4]
            load_w(wab, wab.ap[:, :, 0:128], win[:, c * 128:(c + 1) * 128], KC, 128)
            load_w(wab, wab.ap[:, :, 128:256], win[:, DFF + c * 128:DFF + (c + 1) * 128], KC, 128)
        ldchunk(0)
        ldchunk(1)
        prenorm(s, hb, gf)
        for half in range(2):
            for cl in range(HALF):
                c = half * HALF + cl
                wab = wabs[c % 4]
                if c + 2 < FC:
                    ldchunk(c + 2)
                if cl == 2:
                    load_w(wo, wo.ap, wout[half * HALF * 128:(half + 1) * HALF * 128, :], HALF, D)
                for (c0, n, smp) in subtiles(s):
                    a_ = asb[it % 2]
                    t_ = tb[it % 2]
                    it += 1
                    psa = psum()
                    psb_ = psum()
                    for k in range(KC):
                        mm(psa.ap[:, 0:n], wab.ap[:, k, 0:128], hb.ap[:, k, c0:c0 + n], k == 0, k == KC - 1, [wab, hb], [psa])
                    for k in range(KC):
                        mm(psb_.ap[:, 0:n], wab.ap[:, k, 128:256], hb.ap[:, k, c0:c0 + n], k == 0, k == KC - 1, [wab, hb], [psb_])
                    w0 = t_fdw.ap[:, li, c, 0:1]
                    w1 = t_fdw.ap[:, li, c, 1:2]
                    w2 = t_fdw.ap[:, li, c, 2:3]
                    bb = t_fdb.ap[:, li, c:c + 1]
                    if not smp:
                        cp('pool', a_.ap[:, 0:2], hist.ap[:, c, :], [hist], [a_])
                        cp('act', a_.ap[:, 2:2 + n], psa.ap[:, 0:n], [psa], [a_], grp='a')
                        cp('pool', hist.ap[:, c, :], a_.ap[:, n:n + 2], [a_], [hist])
                        P.op('act', lambda e, o=t_.ap[:, 0:n], i=a_.ap[:, 2:2 + n], sc=w2, b=bb: e.activation(out=o, in_=i, func=AF.Identity, bias=b, scale=sc),
                             [a_, t_fdw, t_fdb], [t_])
                        stt(t_.ap[:, 0:n], a_.ap[:, 1:1 + n], w1, t_.ap[:, 0:n], ALU.mult, ALU.add, [a_, t_, t_fdw], [t_])
                        stt(t_.ap[:, 0:n], a_.ap[:, 0:n], w0, t_.ap[:, 0:n], ALU.mult, ALU.add, [a_, t_, t_fdw], [t_])
                        if s == 1 and c0 == 512:
                            cp('pool', pout.ap[:, c, :], a_.ap[:, n:n + 2], [a_], [pout], grp='po')
                    else:
                        cp('act', sout.ap[:, c, :, 1], psa.ap[:, 0:n], [psa], [sout], grp='so')
                        P.op('act', lambda e, o=t_.ap[:, 0:n], i=psa.ap[:, 0:n], sc=w2, b=bb: e.activation(out=o, in_=i, func=AF.Identity, bias=b, scale=sc),
                             [psa, t_fdw, t_fdb], [t_])
                        stt(t_.ap[:, 0:n], sth.ap[:, c, :, 1], w1, t_.ap[:, 0:n], ALU.mult, ALU.add, [sth, t_, t_fdw], [t_])
                        stt(t_.ap[:, 0:n], sth.ap[:, c, :, 0], w0, t_.ap[:, 0:n], ALU.mult, ALU.add, [sth, t_, t_fdw], [t_])
                    gelu(t_.ap[:, 0:n], t_.ap[:, 0:n], [t_], [t_])
                    tt('dve', gb.ap[:, cl, c0:c0 + n], t_.ap[:, 0:n], psb_.ap[:, 0:n], ALU.mult, [t_, psb_], [gb], grp=('gb', half))
            for m in range(KC):
                for (c0, n, smp) in subtiles(s):
                    ps = psum()
                    for k in range(HALF):
                        mm(ps.ap[:, 0:n], wo.ap[:, k, m * 128:(m + 1) * 128], gb.ap[:, k, c0:c0 + n], k == 0, k == HALF - 1, [wo, gb], [ps])
                    if half == 0:
                        cp('act', yb.ap[:, m, c0:c0 + n], ps.ap[:, 0:n], [ps], [yb], grp=('y0', c0))
                    else:
                        tt('dve', yb.ap[:, m, c0:c0 + n], yb.ap[:, m, c0:c0 + n], ps.ap[:, 0:n], ALU.add, [yb, ps], [yb], grp=('y1', c0))
        if s == 1:
            dma(o_pffn[li], pout.ap, rd=[pout])
            dma(o_sffn[li], sout.ap, rd=[sout])
        postnorm(s, yb, li, 5)
        psrot[0] = 4
        reset()

    def conf(s, li):
        j = li // 2
        psrot[0] = 7
        hb = BA.alloc(KC, W)
        gm = gn_all.ap[:, li, 0, :]
        ub = BA.alloc(KC, 30 + W)
        cbuf = FA.alloc(KC, W)
        fa_keep = FA.off
        hist = conf_hist[j]
        cp('dve', ub.ap[:, :, 0:30], hist.ap, [hist], [ub])
        win = BA.alloc(KC, 2048)
        load_w(win, win.ap, w_conf_in[j], KC, 2048)
        prenorm(s, hb, gm)
        for (c0, n, smp) in subtiles(s):
            fm, bm = FA.off, BA.off
            sg = FA.alloc(512)
            for m in range(KC):
                psa = psum()
                psg = psum()
                for k in range(KC):
                    mm(psa.ap[:, 0:n], win.ap[:, k, m * 128:(m + 1) * 128], hb.ap[:, k, c0:c0 + n], k == 0, k == KC - 1, [win, hb], [psa])
                for k in range(KC):
                    mm(psg.ap[:, 0:n], win.ap[:, k, D + m * 128:D + (m + 1) * 128], hb.ap[:, k, c0:c0 + n], k == 0, k == KC - 1, [win, hb], [psg])
                act(sg.ap[:, 0:n], psg.ap[:, 0:n], AF.Sigmoid, [psg], [sg])
                tt('dve', ub.ap[:, m, 30 + c0:30 + c0 + n], sg.ap[:, 0:n], psa.ap[:, 0:n], ALU.mult, [sg, psa], [ub], grp=('ub', c0))
            FA.off, BA.off = fm, bm
        cp('dve', hist.ap, ub.ap[:, :, STK:STK + 30], [ub], [hist])
        P.barrier()
        BA.off -= KC * 2048
        dgs = [BA.alloc(31, 128) for _ in range(2)]

        def build_dg(m):
            dg = dgs[m % 2]
            for k in range(31):
                if k % 2 == 0:
                    P.op('act', lambda e, o=dg.ap[:, k, :], sc=t_cdw.ap[:, j, m, k:k + 1]: e.activation(out=o, in_=ident_bf.ap, func=AF.Copy, scale=sc),
                         [ident_bf, t_cdw], [dg], 'dg')
                else:
                    ts('dve', dg.ap[:, k, :], ident_bf.ap, t_cdw.ap[:, j, m, k:k + 1], ALU.mult, [ident_bf, t_cdw], [dg], grp='dg')
        build_dg(0)
        for m in range(KC):
            dg = dgs[m % 2]
            if m + 1 < KC:
                build_dg(m + 1)
            for (c0, n, smp) in subtiles(s):
                if smp:
                    continue
                ps = psum()
                for k in range(31):
                    mm(ps.ap[:, 0:n], dg.ap[:, k, :], ub.ap[:, m, c0 + k:c0 + k + n], k == 0, k == 30, [dg, ub], [ps])
                P.op('act', lambda e, o=cbuf.ap[:, m, c0:c0 + n], i=ps.ap[:, 0:n], b=t_cdb.ap[:, j, m:m + 1]: e.activation(out=o, in_=i, func=AF.Identity, bias=b),
                     [ps, t_cdb], [cbuf], ('cb', c0))
        if s == 1:
            ext = FA.alloc(KC, NS, 31)
            dma(ext.ap[:, :, :, 0:30], cfs[j], wr=[ext])
            cp('dve', ext.ap[:, :, :, 30], ub.ap[:, :, 30 + STK:30 + STK + NS], [ub], [ext])
            dma(o_sconf[j], ext.ap[:, :, :, 1:31], rd=[ext])
            pr = FA.alloc(NS, 31)
            cs = FA.alloc(KC, NS)
            for m in range(KC):
                tt('dve', pr.ap, ext.ap[:, m], t_cdw.ap[:, j, m, :].unsqueeze(1).to_broadcast([128, NS, 31]), ALU.mult, [ext, t_cdw], [pr])
                P.op('dve', lambda e, m=m: e.reduce_sum(out=cs.ap[:, m, :], in_=pr.ap, axis=AX.X), [pr], [cs], 'cs')
            for m in range(KC):
                ts('dve', cbuf.ap[:, m, STK:STK + NS], cs.ap[:, m, :], t_cdb.ap[:, j, m:m + 1], ALU.add, [cs, t_cdb], [cbuf], grp='cbs')
            pc = FA.alloc(KC, 30)
            cp('dve', pc.ap, ub.ap[:, :, STK:STK + 30], [ub], [pc])
            dma(o_pconf[j], pc.ap, rd=[pc])
        P.barrier()
        BA.off = bmark
        FA.off = fa_keep
        wo = BA.alloc(KC, D)
        load_w(wo, wo.ap, w_conf_out[j], KC, D)
        yb = cbuf
        for (c0, n, smp) in subtiles(s):
            fm, bm = FA.off, BA.off
            cbb = BA.alloc(KC, 512)
            cp('act', cbb.ap[:, :, 0:n], cbuf.ap[:, :, c0:c0 + n], [cbuf], [cbb])
            psm = psum()
            for c in range(KC):
                mm(psm.ap[:, 0:n], ones_bf.ap, cbb.ap[:, c, 0:n], c == 0, c == KC - 1, [ones_bf, cbb], [psm])
            mu = FA.alloc(512)
            act(mu.ap[:, 0:n], psm.ap[:, 0:n], AF.Copy, [psm], [mu], scale=1.0 / D)
            xc = FA.alloc(KC, 512)
            tt('dve', xc.ap[:, :, 0:n], cbuf.ap[:, :, c0:c0 + n], bc(mu.ap[:, 0:n], KC), ALU.subtract, [cbuf, mu], [xc])
            rs = rstd_of(lambda c: xc.ap[:, c, 0:n], KC, 0, n, 1.0 / D, [xc])
            tt('dve', xc.ap[:, :, 0:n], xc.ap[:, :, 0:n], bc(rs.ap[:, 0:n], KC), ALU.mult, [xc, rs], [xc])
            sb = BA.alloc(KC, 512)
            for c in range(KC):
                P.op('act', lambda e, o=sb.ap[:, c, 0:n], i=xc.ap[:, c, 0:n], sc=t_clg.ap[:, j, c:c + 1], b=t_clb.ap[:, j, c:c + 1]:
                     e.activation(out=o, in_=i, func=AF.Silu, bias=b, scale=sc), [xc, t_clg, t_clb], [sb], 'sb')
            linear(wo, wo.ap, 0, KC, KC, lambda k: sb.ap[:, k, 0:n], [sb], n,
                   lambda m, ps: cp('act', yb.ap[:, m, c0:c0 + n], ps.ap[:, 0:n], [ps], [yb], grp=('yb', c0)))
            FA.off, BA.off = fm, bm
        postnorm(s, yb, li, 1)
        psrot[0] = 4
        reset()

    def ab(s, li):
        j = li // 2
        gm = gn_all.ap[:, li, 0, :]
        hb = BA.alloc(KC, W)
        yab = BA.alloc(KC, W)
        bm_keep = BA.off
        wu = BA.alloc(KC, 512)
        load_w(wu, wu.ap, w_ab_in[j][:, 0:512], KC, 512)
        wg = BA.alloc(4, 512)
        load_w(wg, wg.ap, w_glu[j], 4, 512)
        prenorm(s, hb, gm)
        lr = FA.alloc(16); lim = FA.alloc(16); dt = FA.alloc(16)
        dma(lr.ap, lamr_p[j], wr=[lr]); dma(lim.ap, lami_p[j], wr=[lim]); dma(dt.ap, ldt_p[j], wr=[dt])
        act(dt.ap, dt.ap, AF.Exp, [dt], [dt])
        mag = FA.alloc(16); ang = FA.alloc(16)
        tt('dve', mag.ap, lr.ap, dt.ap, ALU.mult, [lr, dt], [mag])
        act(mag.ap, mag.ap, AF.Exp, [mag], [mag])
        tt('dve', ang.ap, lim.ap, dt.ap, ALU.mult, [lim, dt], [ang])
        NL = 10
        pwc = FA.alloc(NL, 16); pws = FA.alloc(NL, 16)
        c_ = FA.alloc(16); s_ = FA.alloc(16); t1 = FA.alloc(16); t2 = FA.alloc(16)
        hpi = FA.alloc(1)
        memset('dve', hpi, hpi.ap, math.pi / 2)
        act(s_.ap, ang.ap, AF.Sin, [ang], [s_], scale=1.0 / 32)
        act(c_.ap, ang.ap, AF.Sin, [ang, hpi], [c_], scale=1.0 / 32, bias=hpi.ap[:, 0:1])

        def dbl(co, so, ci, si):
            tt('dve', t1.ap, ci, ci, ALU.mult, [c_, pwc], [t1])
            tt('dve', t2.ap, si, si, ALU.mult, [s_, pws], [t2])
            tt('dve', so, ci, si, ALU.mult, [c_, s_, pwc, pws], [s_, pws])
            ts('dve', so, so, 2.0, ALU.mult, [s_, pws], [s_, pws])
            tt('dve', co, t1.ap, t2.ap, ALU.subtract, [t1, t2], [c_, pwc])
        for _ in range(4):
            dbl(c_.ap, s_.ap, c_.ap, s_.ap)
        dbl(pwc.ap[:, 0, :], pws.ap[:, 0, :], c_.ap, s_.ap)
        for l in range(1, NL):
            dbl(pwc.ap[:, l, :], pws.ap[:, l, :], pwc.ap[:, l - 1, :], pws.ap[:, l - 1, :])
        ar = FA.alloc(16); ai = FA.alloc(16); nai = FA.alloc(16)
        tt('dve', ar.ap, mag.ap, pwc.ap[:, 0, :], ALU.mult, [mag, pwc], [ar])
        tt('dve', ai.ap, mag.ap, pws.ap[:, 0, :], ALU.mult, [mag, pws], [ai])
        ts('dve', nai.ap, ai.ap, -1.0, ALU.mult, [ai], [nai])
        den = FA.alloc(16); er = FA.alloc(16); ei = FA.alloc(16); nei = FA.alloc(16); am1 = FA.alloc(16)
        tt('dve', den.ap, lr.ap, lr.ap, ALU.mult, [lr], [den])
        tt('dve', t1.ap, lim.ap, lim.ap, ALU.mult, [lim], [t1])
        tt('dve', den.ap, den.ap, t1.ap, ALU.add, [den, t1], [den])
        P.op('dve', lambda e: e.reciprocal(out=den.ap, in_=den.ap), [den], [den])
        ts('dve', am1.ap, ar.ap, -1.0, ALU.add, [ar], [am1])
        tt('dve', t1.ap, am1.ap, lr.ap, ALU.mult, [am1, lr], [t1])
        tt('dve', t2.ap, ai.ap, lim.ap, ALU.mult, [ai, lim], [t2])
        tt('dve', er.ap, t1.ap, t2.ap, ALU.add, [t1, t2], [er])
        tt('dve', er.ap, er.ap, den.ap, ALU.mult, [er, den], [er])
        tt('dve', t1.ap, ai.ap, lr.ap, ALU.mult, [ai, lr], [t1])
        tt('dve', t2.ap, am1.ap, lim.ap, ALU.mult, [am1, lim], [t2])
        tt('dve', ei.ap, t1.ap, t2.ap, ALU.subtract, [t1, t2], [ei])
        tt('dve', ei.ap, ei.ap, den.ap, ALU.mult, [ei, den], [ei])
        ts('dve', nei.ap, ei.ap, -1.0, ALU.mult, [ei], [nei])
        BTr = BA.alloc(16, 128); BTi = BA.alloc(16, 128); CTr = BA.alloc(16, 128); CTi = BA.alloc(16, 128)
        nCTr = BA.alloc(16, 128); nCTi = BA.alloc(16, 128)
        dgr = FA.alloc(128); dgi = FA.alloc(128); dgn = FA.alloc(128)
        bst = [FA.alloc(2, 128) for _ in range(2)]
        for cc in range(16):
            ts('dve', dgr.ap, ident.ap, er.ap[:, cc:cc + 1], ALU.mult, [ident, er], [dgr])
            ts('dve', dgi.ap, ident.ap, ei.ap[:, cc:cc + 1], ALU.mult, [ident, ei], [dgi])
            ts('dve', dgn.ap, ident.ap, nei.ap[:, cc:cc + 1], ALU.mult, [ident, nei], [dgn])
            bs = bst[cc % 2]
            dma(bs.ap[:, 0, :], brN[j, :, cc, :], wr=[bs])
            dma(bs.ap[:, 1, :], biN[j, :, cc, :], wr=[bs], grp='b2')
            ps = psum()
            mm(ps.ap[:, 0:128], bs.ap[:, 0, :], dgr.ap, True, False, [bs, dgr], [ps])
            mm(ps.ap[:, 0:128], bs.ap[:, 1, :], dgn.ap, False, True, [bs, dgn], [ps])
            mm(ps.ap[:, 128:256], bs.ap[:, 1, :], dgr.ap, True, False, [bs, dgr], [ps])
            mm(ps.ap[:, 128:256], bs.ap[:, 0, :], dgi.ap, False, True, [bs, dgi], [ps])
            cp('act', BTr.ap[:, cc, :], ps.ap[:, 0:128], [ps], [BTr], grp='bt')
            cp('act', BTi.ap[:, cc, :], ps.ap[:, 128:256], [ps], [BTi], grp='bt')
        for (dst, ndst, src) in [(CTr, nCTr, crN), (CTi, nCTi, ciN)]:
            for q4 in range(2):
                st = stage[stcnt[0] % NSTAGE]
                stcnt[0] += 1
                dma(st.ap.rearrange("p (a b) -> p a b", a=8), src[j, :, q4 * 8:(q4 + 1) * 8, :], wr=[st])
                ts('dve', ndst.ap[:, q4 * 8:(q4 + 1) * 8, :], st.ap.rearrange("p (a b) -> p a b", a=8), -1.0, ALU.mult, [st], [ndst], grp='nct')
                cp('act', dst.ap[:, q4 * 8:(q4 + 1) * 8, :], st.ap.rearrange("p (a b) -> p a b", a=8), [st], [dst], grp='ct')
        uf = FA.alloc(4, W)
        ubf = BA.alloc(4, W)
        for (c0, n, smp) in subtiles(s):
            def ev(m, ps, c0=c0, n=n):
                cp('act', uf.ap[:, m, c0:c0 + n], ps.ap[:, 0:n], [ps], [uf], grp=('uf', c0))
                cp('dve', ubf.ap[:, m, c0:c0 + n], ps.ap[:, 0:n], [ps], [ubf], grp=('ubf', c0))
            linear(wu, wu.ap, 0, 4, KC, lambda k: hb.ap[:, k, c0:c0 + n], [hb], n, ev)
        x0r = FA.alloc(16, NS); x0i = FA.alloc(16, NS)
        if s == 1:
            dma(x0r.ap, s5r[j], wr=[x0r])
            dma(x0i.ap, s5i[j], wr=[x0i])
        sor = FA.alloc(16, NS); soi = FA.alloc(16, NS)
        por = FA.alloc(16); poi = FA.alloc(16)
        cosE = [FA.alloc(512) for _ in range(2)]
        sinE = [FA.alloc(512) for _ in range(2)]
        zf = uf
        zb = ubf
        carr, cari = s5car[j]
        _ta = FA.alloc(512); _tb = FA.alloc(512); _tc = FA.alloc(512); _td = FA.alloc(512)
        wk1 = [FA.alloc(512), FA.alloc(512), _ta, _tb, FA.alloc(512), FA.alloc(512)]
        wk2 = [FA.alloc(512), FA.alloc(512), _ta, _tb, FA.alloc(512), FA.alloc(512)]
        tg1 = FA.alloc(256); tg2 = FA.alloc(256)
        tcd = [[_tc, _td], [_tc, _td]]
        wk_ = [wk1, wk2]
        xb_ = [[BA.alloc(512) for _ in range(4)] for _ in range(2)]
        psy = {}
        itn = 0
        def tablegen(cc):
            ce = cosE[cc % 2]
            se = sinE[cc % 2]
            memset('dve', ce, ce.ap[:, 0:1], 1.0)
            memset('dve', se, se.ap[:, 0:1], 0.0)
            yield
            for l in range(9):
                mlen = 1 << l
                cl_ = pwc.ap[:, l, cc:cc + 1]
                sl_ = pws.ap[:, l, cc:cc + 1]
                tmp = tg1
                tmp2 = tg2
                P.op('act', lambda e, o=tmp.ap[:, 0:mlen], i=se.ap[:, 0:mlen], sc=sl_: e.activation(out=o, in_=i, func=AF.Copy, scale=sc), [se, pws], [tmp])
                P.op('act', lambda e, o=tmp2.ap[:, 0:mlen], i=ce.ap[:, 0:mlen], sc=sl_: e.activation(out=o, in_=i, func=AF.Copy, scale=sc), [ce, pws], [tmp2])
                stt(ce.ap[:, mlen:2 * mlen], ce.ap[:, 0:mlen], cl_, tmp.ap[:, 0:mlen], ALU.mult, ALU.subtract, [ce, pwc, tmp], [ce], grp=('ce', cc))
                stt(se.ap[:, mlen:2 * mlen], se.ap[:, 0:mlen], cl_, tmp2.ap[:, 0:mlen], ALU.mult, ALU.add, [se, pwc, tmp2], [se], grp=('se', cc))
                yield
        for _ in tablegen(0):
            pass
        for cc in range(16):
            uc = cc // 4
            ce = cosE[cc % 2]
            se = sinE[cc % 2]
            nxtg = tablegen(cc + 1) if cc + 1 < 16 else iter(())

            def tick():
                next(nxtg, None)
            for (c0, n, smp) in subtiles(s):
                wkk = wk_[itn % 2]
                xbb = xb_[itn % 2]
                itn += 1
                psr = psum(); psi = psum()
                mm(psr.ap[:, 0:n], BTr.ap[:, cc, :], ubf.ap[:, uc, c0:c0 + n], True, True, [BTr, ubf], [psr])
                mm(psi.ap[:, 0:n], BTi.ap[:, cc, :], ubf.ap[:, uc, c0:c0 + n], True, True, [BTi, ubf], [psi])
                p1, p2, p3, p4 = xbb
                if not smp:
                    vr, vi, ta, tb_, yr, yi = wkk
                    tc_, td_ = tcd[itn % 2]
                    tt('dve', ta.ap, psr.ap, ce.ap, ALU.mult, [psr, ce], [ta])
                    tt('dve', tb_.ap, psi.ap, se.ap, ALU.mult, [psi, se], [tb_])
                    tt('pool', vr.ap, ta.ap, tb_.ap, ALU.add, [ta, tb_], [vr])
                    tick()
                    tt('dve', tc_.ap, psi.ap, ce.ap, ALU.mult, [psi, ce], [tc_])
                    tt('dve', td_.ap, psr.ap, se.ap, ALU.mult, [psr, se], [td_])
                    tt('pool', vi.ap, tc_.ap, td_.ap, ALU.subtract, [tc_, td_], [vi])
                    tick()
                    rho = mag.ap[:, cc:cc + 1].to_broadcast([128, 512])
                    P.op('dve', lambda e, o=yr.ap, d1=vr.ap, rho=rho, ini=carr.ap[:, cc:cc + 1]: e.tensor_tensor_scan(out=o, data0=rho, data1=d1, initial=ini, op0=ALU.mult, op1=ALU.add),
                         [vr, mag, carr], [yr])
                    P.op('dve', lambda e, o=yi.ap, d1=vi.ap, rho=rho, ini=cari.ap[:, cc:cc + 1]: e.tensor_tensor_scan(out=o, data0=rho, data1=d1, initial=ini, op0=ALU.mult, op1=ALU.add),
                         [vi, mag, cari], [yi])
                    tick()
                    Rc = pwc.ap[:, 9, cc:cc + 1]
                    Rs = pws.ap[:, 9, cc:cc + 1]
                    ylr = yr.ap[:, 511:512]
                    yli = yi.ap[:, 511:512]
                    tq = FA.alloc(1)
                    if s == 1 and c0 == 512:
                        ts('dve', tq.ap, yli, se.ap[:, 511:512], ALU.mult, [yi, se], [tq])
                        stt(por.ap[:, cc:cc + 1], ylr, ce.ap[:, 511:512], tq.ap, ALU.mult, ALU.subtract, [yr, ce, tq], [por], grp='por')
                        ts('dve', tq.ap, ylr, se.ap[:, 511:512], ALU.mult, [yr, se], [tq])
                        stt(poi.ap[:, cc:cc + 1], yli, ce.ap[:, 511:512], tq.ap, ALU.mult, ALU.add, [yi, ce, tq], [poi], grp='poi')
                    ts('dve', tq.ap, yli, Rs, ALU.mult, [yi, pws], [tq])
                    stt(carr.ap[:, cc:cc + 1], ylr, Rc, tq.ap, ALU.mult, ALU.subtract, [yr, pwc, tq], [carr], grp=('car', cc))
                    ts('dve', tq.ap, ylr, Rs, ALU.mult, [yr, pws], [tq])
                    stt(cari.ap[:, cc:cc + 1], yli, Rc, tq.ap, ALU.mult, ALU.add, [yi, pwc, tq], [cari], grp=('cai', cc))
                    FA.off -= 16
                    tick()
                    tt('pool', p1.ap, yr.ap, ce.ap, ALU.mult, [yr, ce], [p1])
                    tt('pool', p2.ap, yi.ap, se.ap, ALU.mult, [yi, se], [p2])
                    tt('pool', p3.ap, yr.ap, se.ap, ALU.mult, [yr, se], [p3])
                    tt('pool', p4.ap, yi.ap, ce.ap, ALU.mult, [yi, ce], [p4])
                    tick()
                else:
                    ta, tb_ = wkk[2], wkk[3]
                    a_r = ar.ap[:, cc:cc + 1]; a_i = ai.ap[:, cc:cc + 1]; na_i = nai.ap[:, cc:cc + 1]
                    ts('dve', ta.ap[:, 0:n], x0r.ap[:, cc, :], a_r, ALU.mult, [x0r, ar], [ta])
                    stt(ta.ap[:, 0:n], x0i.ap[:, cc, :], na_i, ta.ap[:, 0:n], ALU.mult, ALU.add, [x0i, nai, ta], [ta])
                    tt('dve', sor.ap[:, cc, :], ta.ap[:, 0:n], psr.ap[:, 0:n], ALU.add, [ta, psr], [sor], grp='sor')
                    ts('dve', tb_.ap[:, 0:n], x0i.ap[:, cc, :], a_r, ALU.mult, [x0i, ar], [tb_])
                    stt(tb_.ap[:, 0:n], x0r.ap[:, cc, :], a_i, tb_.ap[:, 0:n], ALU.mult, ALU.add, [x0r, ai, tb_], [tb_])
                    tt('dve', soi.ap[:, cc, :], tb_.ap[:, 0:n], psi.ap[:, 0:n], ALU.add, [tb_, psi], [soi], grp='soi')
                    cp('act', p1.ap[:, 0:n], sor.ap[:, cc, :], [sor], [p1])
                    cp('act', p4.ap[:, 0:n], soi.ap[:, cc, :], [soi], [p4])
                py = psfix({0: 0, 512: 1, STK: 2}[c0])
                mm(py.ap[:, 0:n], CTr.ap[:, cc, :], p1.ap[:, 0:n], cc % 4 == 0, False, [CTr, p1], [py])
                if not smp:
                    mm(py.ap[:, 0:n], nCTr.ap[:, cc, :], p2.ap[:, 0:n], False, False, [nCTr, p2], [py])
                    mm(py.ap[:, 0:n], nCTi.ap[:, cc, :], p3.ap[:, 0:n], False, False, [nCTi, p3], [py])
                mm(py.ap[:, 0:n], nCTi.ap[:, cc, :], p4.ap[:, 0:n], False, cc % 4 == 3, [nCTi, p4], [py])
                if cc % 4 == 3:
                    stt(zf.ap[:, uc, c0:c0 + n], uf.ap[:, uc, c0:c0 + n], t_s5d.ap[:, j, uc:uc + 1], py.ap[:, 0:n], ALU.mult, ALU.add,
                        [uf, t_s5d, py], [zf], grp=('zf', c0))
            for _ in nxtg:
                pass
        if s == 1:
            dma(o_ss5r[j], sor.ap, rd=[sor]); dma(o_ss5i[j], soi.ap, rd=[soi])
            dma(o_ps5r[j], por.ap, rd=[por]); dma(o_ps5i[j], poi.ap, rd=[poi])
        for (c0, n, smp) in subtiles(s):
            gelu(zf.ap[:, :, c0:c0 + n], zf.ap[:, :, c0:c0 + n], [zf], [zf], grp=('zg', c0))
            cp('dve', zb.ap[:, :, c0:c0 + n], zf.ap[:, :, c0:c0 + n], [zf], [zb], grp=('zb', c0))
        for (c0, n, smp) in subtiles(s):
            def ev2(m, ps, c0=c0, n=n):
                sg = wk_[m % 2][0]
                act(sg.ap[:, 0:n], ps.ap[:, 0:n], AF.Sigmoid, [ps, t_bglu], [sg], bias=t_bglu.ap[:, j, m:m + 1])
                tt('dve', yab.ap[:, m, c0:c0 + n], zf.ap[:, m, c0:c0 + n], sg.ap[:, 0:n], ALU.mult, [zf, sg], [yab], grp=('ya', c0))
            linear(wg, wg.ap, 0, 4, 4, lambda k: zb.ap[:, k, c0:c0 + n], [zb], n, ev2)
        P.barrier()
        FA.off = fmark
        BA.off = bm_keep
        wh = BA.alloc(KC, 2048)
        load_w(wh, wh.ap, w_ab_in[j][:, 512:2560], KC, 2048, gain=gm)
        S = hg_S[j]
        Sb = [BA.alloc(128) for _h in range(4)]
        for _h in range(4):
            cp('act', Sb[_h].ap, S[_h].ap, [S[_h]], [Sb[_h]])
        lbp = lb_p.ap[:, j, :]
        omlbp = omlb_p.ap[:, j, :]
        ones512 = FA.alloc(512)
        memset('dve', ones512, ones512.ap, 1.0)
        for (c0, n, smp) in subtiles(s):
            fm, bm = FA.off, BA.off
            if not smp:
                nblk = n // 128
                vtm = BA.alloc(nblk, 512)
                for blk in range(nblk):
                    ps = psum()
                    for k in range(KC):
                        mm(ps.ap, hb.ap[:, k, c0 + blk * 128:c0 + (blk + 1) * 128], wh.ap[:, k, 1024:1536], k == 0, k == KC - 1, [hb, wh], [ps])
                    cp('act', vtm.ap[:, blk, :], ps.ap, [ps], [vtm], grp='vtm')
                sgo_a = [FA.alloc(512) for _h in range(4)]
                Dc_a = [FA.alloc(16) for _h in range(4)]
                ob_all = FA.alloc(4, 512)
                Qt_a = [BA.alloc(512) for _h in range(4)]
                Kt_a = [BA.alloc(512) for _h in range(4)]
                Kh_a = [BA.alloc(512) for _h in range(4)]
                khm_a = [BA.alloc(4, 128) for _h in range(4)]
                atm_a = [BA.alloc(128) for _h in range(4)]
                qs = FA.alloc(512); f_ = FA.alloc(512); lf = FA.alloc(512); k_ = FA.alloc(512); G = FA.alloc(512)
                A1 = FA.alloc(512); A2 = FA.alloc(512)
                Gs = FA.alloc(16)
                Ep = FA.alloc(512); Em = FA.alloc(512); Eh = FA.alloc(512)
                for h in range(4):
                    sgo = sgo_a[h]; Dc = Dc_a[h]; Qt = Qt_a[h]; Kt = Kt_a[h]; Kh = Kh_a[h]
                    psq = psum(); psf = psum(); psg = psum()
                    for k in range(KC):
                        mm(psq.ap[:, 0:n], wh.ap[:, k, h * 128:(h + 1) * 128], hb.ap[:, k, c0:c0 + n], k == 0, k == KC - 1, [wh, hb], [psq])
                    for k in range(KC):
                        mm(psf.ap[:, 0:n], wh.ap[:, k, 512 + h * 128:512 + (h + 1) * 128], hb.ap[:, k, c0:c0 + n], k == 0, k == KC - 1, [wh, hb], [psf])
                    for k in range(KC):
                        mm(psg.ap[:, 0:n], wh.ap[:, k, 1536 + h * 128:1536 + (h + 1) * 128], hb.ap[:, k, c0:c0 + n], k == 0, k == KC - 1, [wh, hb], [psg])
                    act(qs.ap, psq.ap, AF.Sigmoid, [psq], [qs])
                    act(sgo.ap, psg.ap, AF.Sigmoid, [psg], [sgo])
                    act(f_.ap, psf.ap, AF.Sigmoid, [psf], [f_])
                    tt('dve', qs.ap, qs.ap, psq.ap, ALU.mult, [qs, psq], [qs])
                    ts('dve', k_.ap, f_.ap, omlbp[:, h:h + 1], ALU.mult, [f_, omlb_p], [k_], s2=-1.0, op1=ALU.mult)
                    ts('dve', k_.ap, k_.ap, omlbp[:, h:h + 1], ALU.add, [k_, omlb_p], [k_])
                    ts('dve', f_.ap, f_.ap, omlbp[:, h:h + 1], ALU.mult, [f_, omlb_p, lb_p], [f_], s2=lbp[:, h:h + 1], op1=ALU.add)
                    act(lf.ap, f_.ap, AF.Ln, [f_], [lf])
                    P.op('dve', lambda e, o=G.ap, d1=lf.ap: e.tensor_tensor_scan(out=o, data0=ones512.ap, data1=d1, initial=0.0, op0=ALU.mult, op1=ALU.add),
                         [lf, ones512], [G])
                    G3 = G.ap.rearrange("p (a b) -> p a b", b=32)
                    memset('dve', Gs, Gs.ap[:, 0:1], 0.0)
                    cp('dve', Gs.ap[:, 1:16], G3[:, 0:15, 31], [G], [Gs], grp='gs')
                    tt('dve', A1.ap.rearrange("p (a b) -> p a b", b=32), G3, Gs.ap.unsqueeze(2).to_broadcast([128, 16, 32]), ALU.subtract, [G, Gs], [A1])
                    tt('dve', A2.ap.rearrange("p (a b) -> p a b", b=32), G3, G3[:, :, 31:32].to_broadcast([128, 16, 32]), ALU.subtract, [G], [A2])
                    tt('dve', Dc.ap, G3[:, :, 31], Gs.ap, ALU.subtract, [G, Gs], [Dc])
                    act(Dc.ap, Dc.ap, AF.Exp, [Dc], [Dc])
                    act(Ep.ap, A1.ap, AF.Exp, [A1], [Ep])
                    act(Em.ap, A1.ap, AF.Exp, [A1], [Em], scale=-1.0)
                    act(Eh.ap, A2.ap, AF.Exp, [A2], [Eh], scale=-1.0)
                    tt('pool', Qt.ap, qs.ap, Ep.ap, ALU.mult, [qs, Ep], [Qt])
                    tt('pool', Kt.ap, k_.ap, Em.ap, ALU.mult, [k_, Em], [Kt])
                    tt('pool', Kh.ap, k_.ap, Eh.ap, ALU.mult, [k_, Eh], [Kh])
                for blk in range(nblk):
                    b0 = blk * 128
                    pso = psfix(0)
                    psn = psfix(1)
                    for h in range(4):
                        Qt = Qt_a[h]; Kt = Kt_a[h]; Kh = Kh_a[h]; khm = khm_a[h]; atm = atm_a[h]
                        psa = psum()
                        mm(psa.ap[:, 0:128], Kt.ap[:, b0:b0 + 128], Qt.ap[:, b0:b0 + 128], True, True, [Kt, Qt], [psa])
                        tt('dve', atm.ap, psa.ap[:, 0:128], mask16.ap, ALU.mult, [psa, mask16], [atm])
                        P.op('pe', lambda e, o=psbf.ap[:, 512:640], i=Kh.ap[:, b0:b0 + 128]: e.transpose(out=o, in_=i, identity=ident_bf.ap), [Kh, ident_bf], [psbf])
                        tt('dve', khm.ap, psbf.ap[:, 512:640].unsqueeze(1).to_broadcast([128, 4, 128]), cmask.ap.unsqueeze(2).to_broadcast([128, 4, 128]), ALU.mult, [psbf, cmask], [khm])
                        mm(pso.ap[:, h * 128:(h + 1) * 128], vtm.ap[:, blk, h * 128:(h + 1) * 128], atm.ap, True, True, [vtm, atm], [pso])
                    for c in range(4):
                        for h in range(4):
                            mm(psn.ap[:, h * 128 + 32 * c:h * 128 + 32 * c + 32], Sb[h].ap, Qt_a[h].ap[:, b0 + 32 * c:b0 + 32 * c + 32], True, True, [Sb[h], Qt_a[h]], [psn])
                            psu = psum()
                            mm(psu.ap[:, 0:128], khm_a[h].ap[:, c, :], vtm.ap[:, blk, h * 128:(h + 1) * 128], True, True, [khm_a[h], vtm], [psu])
                            stt(S[h].ap, S[h].ap, Dc_a[h].ap[:, blk * 4 + c:blk * 4 + c + 1], psu.ap[:, 0:128], ALU.mult, ALU.add, [S[h], Dc_a[h], psu], [S[h]])
                            cp('act', Sb[h].ap, S[h].ap, [S[h]], [Sb[h]])
                    cp('act', ob_all.ap[:, :, b0:b0 + 128], pso.ap.rearrange("p (a b) -> p a b", a=4), [pso], [ob_all], grp='ob')
                    tt('dve', ob_all.ap[:, :, b0:b0 + 128], ob_all.ap[:, :, b0:b0 + 128], psn.ap.rearrange("p (a b) -> p a b", a=4), ALU.add, [ob_all, psn], [ob_all], grp='ob2')
                for h in range(4):
                    fm2, bm2 = FA.off, BA.off
                    rs = rstd_of(lambda c, h=h: ob_all.ap[:, h, 0:n], 1, 0, n, 1.0 / 128, [ob_all])
                    tmpo = FA.alloc(512)
                    tt('dve', tmpo.ap, ob_all.ap[:, h, :], rs.ap[:, 0:n], ALU.mult, [ob_all, rs], [tmpo])
                    stt(yab.ap[:, 4 + h, c0:c0 + n], tmpo.ap, t_gnp.ap[:, j, h:h + 1], sgo_a[h].ap, ALU.mult, ALU.mult, [tmpo, t_gnp, sgo_a[h]], [yab], grp=('yb', c0))
                    FA.off, BA.off = fm2, bm2
                if s == 1 and c0 == 512:
                    for _h in range(4):
                        dma(o_phg[j, _h], S[_h].ap, rd=[S[_h]])
            else:
                lbb = FA.alloc(512)
                dma(lbb.ap[0:NS, :], lbl_b[1:2, :].to_broadcast([NS, 512]) if False else lbl_b[1, :].partition_broadcast(NS), wr=[lbb])
                lb0 = FA.alloc(512)
                dma(lb0.ap[0:NS, :], lbl_b[0, :].partition_broadcast(NS), wr=[lb0])
                psft = psum(); psvt = psum()
                for k in range(KC):
                    mm(psft.ap[0:NS, :], hb.ap[:, k, c0:c0 + n], wh.ap[:, k, 512:1024], k == 0, k == KC - 1, [hb, wh], [psft])
                for k in range(KC):
                    mm(psvt.ap[0:NS, :], hb.ap[:, k, c0:c0 + n], wh.ap[:, k, 1024:1536], k == 0, k == KC - 1, [hb, wh], [psvt])
                ktm = FA.alloc(512); vt_ = FA.alloc(512); lbt = FA.alloc(512)
                if j == 0:
                    memset('dve', lbt, lbt.ap[0:NS, :], 0.0)
                else:
                    tt('dve', lbt.ap[0:NS, :], lbb.ap[0:NS, :], lb0.ap[0:NS, :], ALU.subtract, [lbb, lb0], [lbt])
                    act(lbt.ap[0:NS, :], lbt.ap[0:NS, :], AF.Sigmoid, [lbt], [lbt])
                act(ktm.ap[0:NS, :], psft.ap[0:NS, :], AF.Sigmoid, [psft], [ktm])
                ts('dve', ktm.ap[0:NS, :], ktm.ap[0:NS, :], -1.0, ALU.mult, [ktm], [ktm], s2=1.0, op1=ALU.add)
                ts('dve', lbt.ap[0:NS, :], lbt.ap[0:NS, :], -1.0, ALU.mult, [lbt], [lbt], s2=1.0, op1=ALU.add)
                tt('dve', ktm.ap[0:NS, :], ktm.ap[0:NS, :], lbt.ap[0:NS, :], ALU.mult, [ktm, lbt], [ktm])
                cp('act', vt_.ap[0:NS, :], psvt.ap[0:NS, :], [psvt], [vt_])
                kms = [FA.alloc(512) for _ in range(2)]
                qs = FA.alloc(4, NS); f_ = FA.alloc(4, NS); sgo = FA.alloc(4, NS)
                for h in range(4):
                    psq = psum(); psf = psum(); psg = psum()
                    for k in range(KC):
                        mm(psq.ap[:, 0:n], wh.ap[:, k, h * 128:(h + 1) * 128], hb.ap[:, k, c0:c0 + n], k == 0, k == KC - 1, [wh, hb], [psq])
                    for k in range(KC):
                        mm(psf.ap[:, 0:n], wh.ap[:, k, 512 + h * 128:512 + (h + 1) * 128], hb.ap[:, k, c0:c0 + n], k == 0, k == KC - 1, [wh, hb], [psf])
                    for k in range(KC):
                        mm(psg.ap[:, 0:n], wh.ap[:, k, 1536 + h * 128:1536 + (h + 1) * 128], hb.ap[:, k, c0:c0 + n], k == 0, k == KC - 1, [wh, hb], [psg])
                    act(qs.ap[:, h, :], psq.ap[:, 0:n], AF.Silu, [psq], [qs], grp='qs')
                    act(sgo.ap[:, h, :], psg.ap[:, 0:n], AF.Sigmoid, [psg], [sgo], grp='sgo')
                    act(f_.ap[:, h, :], psf.ap[:, 0:n], AF.Sigmoid, [psf], [f_], grp='f1')
                    ts('dve', f_.ap[:, h, :], f_.ap[:, h, :], omlbp[:, h:h + 1], ALU.mult, [f_, omlb_p, lb_p], [f_], s2=lbp[:, h:h + 1], op1=ALU.add, grp='f2')
                s0s = [FA.alloc(4, 128) for _ in range(2)]
                sns = [FA.alloc(4, 128) for _ in range(2)]
                pso = psfix(2)
                for nn in range(NS):
                    s0 = s0s[nn % 2]
                    sn = sns[nn % 2]
                    dma(s0.ap, hgs[j, nn].rearrange("h d v -> d h v"), wr=[s0])
                    psu = psum()
                    km = kms[nn % 2]
                    ts('dve', km.ap[0:NS, :], ktm.ap[0:NS, :], ident.ap[0:NS, nn:nn + 1], ALU.mult, [ktm, ident], [km])
                    for h in range(4):
                        mm(psu.ap[:, h * 128:(h + 1) * 128], km.ap[0:NS, h * 128:(h + 1) * 128], vt_.ap[0:NS, h * 128:(h + 1) * 128], True, True, [km, vt_], [psu])
                    for h in range(4):
                        stt(sn.ap[:, h, :], s0.ap[:, h, :], f_.ap[:, h, nn:nn + 1], psu.ap[:, h * 128:(h + 1) * 128], ALU.mult, ALU.add, [s0, f_, psu], [sn], grp=('sn', nn))
                    dma(o_shg[j, nn].rearrange("h d v -> d h v"), sn.ap, rd=[sn])
                    for h in range(4):
                        mm(pso.ap[:, h * NS + nn:h * NS + nn + 1], sn.ap[:, h, :], qs.ap[:, h, nn:nn + 1], True, True, [sn, qs], [pso])
                ob = FA.alloc(4, NS)
                cp('act', ob.ap, pso.ap[:, 0:4 * NS].rearrange("p (a b) -> p a b", a=4), [pso], [ob])
                for h in range(4):
                    rs = rstd_of(lambda c: ob.ap[:, h, :], 1, 0, n, 1.0 / 128, [ob])
                    tt('pool', ob.ap[:, h, :], ob.ap[:, h, :], rs.ap[:, 0:n], ALU.mult, [ob, rs], [ob])
                    stt(yab.ap[:, 4 + h, c0:c0 + n], ob.ap[:, h, :], t_gnp.ap[:, j, h:h + 1], sgo.ap[:, h, :], ALU.mult, ALU.mult, [ob, t_gnp, sgo], [yab], grp=('yb', c0))
            FA.off, BA.off = fm, bm
        P.barrier()
        FA.off = fmark
        BA.off = bm_keep
        wo = BA.alloc(KC, D)
        load_w(wo, wo.ap, w_ab_out[j], KC, D)
        yb = FA.alloc(KC, W)
        for (c0, n, smp) in subtiles(s):
            linear(wo, wo.ap, 0, KC, KC, lambda k: yab.ap[:, k, c0:c0 + n], [yab], n,
                   lambda m, ps: cp('act', yb.ap[:, m, c0:c0 + n], ps.ap[:, 0:n], [ps], [yb], grp=('yb', c0)))
        postnorm(s, yb, li, 1)
        reset()

    mem_prep()
    stop = (STOP_AFTER == 0)
    for s in range(NSUP):
        if stop:
            break
        dma(x.ap[:, :, 0:STK], xT.rearrange("(c p) t -> p c t", p=128)[:, :, s * STK:(s + 1) * STK], wr=[x])
        if s == 1:
            dma(x.ap[:, :, STK:W], xsT.rearrange("(c p) t -> p c t", p=128), wr=[x], grp='xs')
        for li in range(DEPTH):
            if li % 2 == 0:
                ab(s, li)
            else:
                conf(s, li)
            if done():
                stop = True
                break
            xattn(s, li)
            if done():
                stop = True
                break
            ffn(s, li)
            if done():
                stop = True
                break
        dma(yT.rearrange("(c p) t -> p c t", p=128)[:, :, s * STK:(s + 1) * STK], x.ap[:, :, 0:STK], rd=[x])
        if s == 1:
            dma(ysT.rearrange("(c p) t -> p c t", p=128), x.ap[:, :, STK:W], rd=[x])

    blk = enter(nc.Block())
    P.emit(nc, blk, sems, dsems)
    for c in reversed(ctx):
        c.__exit__(None, None, None)
    print("arena peak f32", FA.peak, "bf16", BA.peak, "ops", {e: len(P.ops[e]) for e in ENGS})
    return nc


def _pad_lhsT(a, mode):
    out = np.zeros((128, 16, 128), np.float32)
    for g in range(32):
        cc, two = g // 2, g % 2
        ccl = cc % 4
        blk = a[g] if mode == 'b' else a[g].T
        out[two * 64:(two + 1) * 64, cc, ccl * 32 + two * 16: ccl * 32 + two * 16 + 16] = blk
    return out


_NC_CACHE = {}


def kernel(**inp):
    f = lambda a: np.ascontiguousarray(np.asarray(a, dtype=np.float32))
    if 'nc' not in _NC_CACHE:
        _NC_CACHE['nc'] = build()
    nc = _NC_CACHE['nc']
    shared = {}
    for k in ['w_ab_in', 'w_ab_out', 'w_conf_in', 'w_conf_out', 'w_xq', 'w_xk', 'w_xv', 'w_xo', 'w_ffn_in', 'w_ffn_out']:
        shared[k] = f(inp[k])
    shared['w_glu'] = f(inp['s5_w_glu'])
    shared['gains'] = f(np.asarray(inp['norm_gains']).reshape(DEPTH, 7, 8, 128).transpose(3, 0, 1, 2))

    def pl(a):
        return f(np.asarray(a).reshape(2, 16, 2, 64).transpose(0, 2, 3, 1).reshape(2, 128, 16))
    shared['lamr_p'] = pl(inp['s5_lambda_re'])
    shared['lami_p'] = pl(inp['s5_lambda_im'])
    shared['ldt_p'] = pl(np.repeat(np.asarray(inp['s5_log_dt'])[:, :, None], 64, axis=2))
    shared['brN'] = f(np.stack([_pad_lhsT(np.asarray(inp['s5_b_re'])[j], 'b') for j in range(2)]))
    shared['biN'] = f(np.stack([_pad_lhsT(np.asarray(inp['s5_b_im'])[j], 'b') for j in range(2)]))
    shared['crN'] = f(np.stack([_pad_lhsT(np.asarray(inp['s5_c_re'])[j], 'c') for j in range(2)]))
    shared['ciN'] = f(np.stack([_pad_lhsT(np.asarray(inp['s5_c_im'])[j], 'c') for j in range(2)]))

    def pc(a, nch):
        a = np.asarray(a)
        return f(a.reshape(a.shape[0], nch, 128).transpose(2, 0, 1))
    shared['s5d'] = pc(inp['s5_d'], 4)
    shared['bglu'] = pc(inp['s5_b_glu'], 4)
    shared['lbl_p'] = pc(inp['hg_lb_logits'], 4)
    shared['lbl_b'] = f(inp['hg_lb_logits'])
    shared['gnp'] = pc(inp['hg_gnorm'], 4)
    shared['cdw'] = f(np.asarray(inp['conf_dw']).reshape(2, 31, 8, 128).transpose(3, 0, 2, 1))
    shared['cdb'] = pc(inp['conf_dw_b'], 8)
    shared['clg'] = pc(inp['conf_ln_g'], 8)
    shared['clb'] = pc(inp['conf_ln_b'], 8)
    shared['fdw'] = f(np.asarray(inp['ffn_dw']).reshape(DEPTH, 3, FC, 128).transpose(3, 0, 2, 1))
    shared['fdb'] = pc(inp['ffn_dw_b'], FC)
    shared['c_ident'] = np.eye(128, dtype=np.float32)
    ss_, tt_ = np.meshgrid(np.arange(128), np.arange(128), indexing='ij')
    shared['c_mask'] = ((tt_ >= ss_) & (tt_ // 32 == ss_ // 32)).astype(np.float32)
    shared['c_cmask'] = (np.arange(128)[:, None] // 32 == np.arange(4)[None, :]).astype(np.float32)

    xp = np.asarray(inp['x_prompt']); xs = np.asarray(inp['x_sample']); mp = np.asarray(inp['mem_prompt'])
    ck = np.asarray(inp['cache_mem_k']); cv = np.asarray(inp['cache_mem_v'])
    sr = np.asarray(inp['state_s5_re']); si = np.asarray(inp['state_s5_im'])
    hg = np.asarray(inp['state_hgrn']); cf = np.asarray(inp['state_conf']); ff = np.asarray(inp['state_ffn'])
    in_maps = []
    for c in range(NCORE):
        sl = slice(c * NS, (c + 1) * NS)
        m = dict(shared)
        m['xT'] = f(xp[c].T)
        m['xsT'] = f(xs[sl, 0, :].T)
        m['memT'] = f(mp[c].T)
        m['kcT'] = f(ck[:, sl].transpose(0, 1, 3, 4, 2))
        m['vc'] = f(cv[:, sl].reshape(DEPTH, NS, 256, D))

        def s5l(a):
            return f(a[:, sl].reshape(2, NS, 16, 2, 64).transpose(0, 3, 4, 2, 1).reshape(2, 128, 16, NS))
        m['s5r'] = s5l(sr)
        m['s5i'] = s5l(si)
        m['hgs'] = f(hg[:, sl])
        m['cfs'] = f(cf[:, sl].reshape(2, NS, 30, 8, 128).transpose(0, 4, 3, 1, 2))
        m['ffs'] = f(ff[:, sl].reshape(DEPTH, NS, 2, FC, 128).transpose(0, 4, 3, 1, 2))
        in_maps.append(m)
    res = run_bass_kernel_spmd(nc, in_maps[:NRUN], core_ids=list(range(NRUN)))
    R = res.results
    g = lambda name: np.stack([np.asarray(R[min(c, NRUN - 1)][name]) for c in range(NCORE)])
    y_prompt = g('yT').transpose(0, 2, 1)
    y_sample = g('ysT').transpose(0, 2, 1).reshape(NCORE * NS, 1, D)

    def ps5(a):
        return a.reshape(NCORE, 2, 2, 64, 16).transpose(1, 0, 4, 2, 3).reshape(2, NCORE, 32, 64)
    p_re = ps5(g('o_ps5r')); p_im = ps5(g('o_ps5i'))
    p_hg = g('o_phg').transpose(1, 0, 2, 3, 4)
    p_conf = g('o_pconf').transpose(1, 0, 4, 3, 2).reshape(2, NCORE, 30, D)
    p_ffn = g('o_pffn').transpose(1, 0, 4, 3, 2).reshape(DEPTH, NCORE, 2, DFF)
    p_mk = g('o_pmk').transpose(1, 0, 2, 3).reshape(DEPTH, NCORE, NMEM, 4, 256)
    p_mv = g('o_pmv').transpose(1, 0, 2, 3).reshape(DEPTH, NCORE, NMEM, 4, 256)

    def ss5(a):
        return a.reshape(NCORE, 2, 2, 64, 16, NS).transpose(1, 0, 5, 4, 2, 3).reshape(2, NCORE * NS, 32, 64)
    s_re = ss5(g('o_ss5r')); s_im = ss5(g('o_ss5i'))
    s_hg = g('o_shg').transpose(1, 0, 2, 3, 4, 5).reshape(2, NCORE * NS, 4, 128, 128)
    s_conf = g('o_sconf').transpose(1, 0, 4, 5, 3, 2).reshape(2, NCORE * NS, 30, D)
    s_ffn = g('o_sffn').transpose(1, 0, 4, 5, 3, 2).reshape(DEPTH, NCORE * NS, 2, DFF)
    outs = (y_prompt, y_sample, p_re, p_im, p_hg, p_conf, p_ffn, p_mk, p_mv, s_re, s_im, s_hg, s_conf, s_ffn)
    return tuple(np.ascontiguousarray(o, dtype=np.float32) for o in outs)
```

```python
import math
import bisect
import numpy as np
import concourse.bass as bass
import concourse.mybir as mybir
from concourse.bass_utils import run_bass_kernel_spmd

F32 = mybir.dt.float32
BF16 = mybir.dt.bfloat16
ALU = mybir.AluOpType
AF = mybir.ActivationFunctionType
AX = mybir.AxisListType

NCORE = 8
D = 1024
KC = 8
DEPTH = 4
TP = 2048
STK = 1024
NSUP = 2
NS = 16
W = STK + NS
DFF = 2816
FC = 22
NMEM = 256
EPS = 1e-6
NDMASEM = 16
NSTAGE = 4
ENGS = ['pe', 'dve', 'act', 'pool', 'sp']
STOP_AFTER = None
DBG = set()
NRUN = 8
USE_BARRIER = False
NF_ = 32200
NB_ = 48000


class Seg:
    __slots__ = ('lw', 'lwgrp', 'rd', 'prd', 'plw')

    def __init__(self, o=None):
        cpd = lambda d: {k: (list(v) if isinstance(v, list) else v) for k, v in d.items()}
        if o is None:
            self.lw = {}
            self.lwgrp = None
            self.rd = {}
            self.prd = {}
            self.plw = {}
        else:
            self.lw = cpd(o.lw)
            self.lwgrp = o.lwgrp
            self.rd = cpd(o.rd)
            self.prd = cpd(o.prd)
            self.plw = cpd(o.plw)


class Space:
    def __init__(self, n):
        self.n = n
        self.bounds = [0]
        self.st = {0: Seg()}

    def split(self, x):
        if x <= 0 or x >= self.n:
            return
        i = bisect.bisect_right(self.bounds, x) - 1
        a = self.bounds[i]
        if a == x:
            return
        self.bounds.insert(i + 1, x)
        self.st[x] = Seg(self.st[a])

    def segs(self, a, b):
        i = bisect.bisect_left(self.bounds, a)
        out = []
        while i < len(self.bounds) and self.bounds[i] < b:
            out.append(self.st[self.bounds[i]])
            i += 1
        return out


class Buf:
    def __init__(self, ap, space=None, a=0, b=1):
        self.ap = ap
        self.space = space if space is not None else Space(1)
        self.a = a
        self.b = b
        self.excl = False

    def segs(self):
        return self.space.segs(self.a, self.b)

    def __getitem__(self, k):
        return self.ap[k]


def _add(d, me):
    e, i = me
    if e == 'sp':
        d.setdefault('sp', []).append(i)
    else:
        d[e] = max(d.get(e, -1), i)


def _items(d):
    for e, v in d.items():
        if e == 'sp':
            for i in v:
                yield ('sp', i)
        else:
            yield (e, v)


class Prog:
    def __init__(self):
        self.ops = {e: [] for e in ENGS}

    def op(self, eng, fn, rd=(), wr=(), grp=None):
        idx = len(self.ops[eng])
        me = (eng, idx)
        deps = set()
        wr = list(wr) + [b for b in rd if b.excl and b not in wr]
        rd = [b for b in rd if not b.excl]
        rds = [g for b in rd for g in b.segs()]
        wrs = [g for b in wr for g in b.segs()]
        for b in rds:
            deps.update(_items(b.lw))
        for b in wrs:
            if grp is not None and b.lwgrp == grp:
                for it in _items(b.rd):
                    _add(b.prd, it)
            else:
                b.prd = b.rd
                b.plw = b.lw
                b.lw = {}
                b.lwgrp = grp
            b.rd = {}
            deps.update(_items(b.prd))
            deps.update(_items(b.plw))
        if eng == 'pe':
            deps = {d for d in deps if d[0] != 'pe'}
        deps.discard(me)
        for b in rds:
            _add(b.rd, me)
        for b in wrs:
            _add(b.lw, me)
        self.ops[eng].append((fn, deps))

    def barrier(self):
        if not USE_BARRIER:
            return
        last = []
        for e in ENGS:
            i = len(self.ops[e]) - 1
            while i >= 0 and self.ops[e][i][0] is None:
                i -= 1
            if i >= 0:
                last.append((e, i))
        for e in ENGS:
            if e == 'sp':
                continue
            deps = {d for d in last if d[0] != e}
            deps.update(('sp', i) for i in range(max(0, len(self.ops['sp']) - NDMASEM), len(self.ops['sp'])))
            self.ops[e].append((None, deps))

    def emit(self, nc, block, sems, dsems):
        flagged = {e: set() for e in ENGS}
        for e in ENGS:
            for fn, deps in self.ops[e]:
                for d in deps:
                    flagged[d[0]].add(d[1])
        cnt = {e: {} for e in ENGS}
        for e in ENGS:
            c = 0
            for i in range(len(self.ops[e])):
                if i in flagged[e]:
                    c += 1
                cnt[e][i] = c
        K = NDMASEM

        def run(ename, eng):
            known = {}

            def wait(key, semh, val):
                if known.get(key, 0) < val:
                    eng.wait_ge(semh, val)
                    known[key] = val
            for i, (fn, deps) in enumerate(self.ops[ename]):
                for d in sorted(deps):
                    if d[0] == 'sp':
                        wait(('d', d[1] % K), dsems[d[1] % K], 16 * (d[1] // K + 1))
                    else:
                        wait(d[0], sems[d[0]], cnt[d[0]][d[1]])
                if ename == 'sp' and i >= K:
                    wait(('d', i % K), dsems[i % K], 16 * (i // K))
                if fn is None:
                    if i in flagged[ename]:
                        eng.sem_inc(sems[ename], 1)
                    continue
                ins = fn(eng)
                if ename == 'sp':
                    ins.then_inc(dsems[i % K], 16)
                elif i in flagged[ename]:
                    ins.then_inc(sems[ename], 1)
            if ename == 'sp':
                n = len(self.ops['sp'])
                for k in range(K):
                    uses = (n - k + K - 1) // K if n > k else 0
                    if uses:
                        wait(('d', k), dsems[k], 16 * uses)

        block.tensor(lambda e: run('pe', e))
        block.vector(lambda e: run('dve', e))
        block.scalar(lambda e: run('act', e))
        block.gpsimd(lambda e: run('pool', e))
        block.sync(lambda e: run('sp', e))


class Arena:
    def __init__(self, ap, n):
        self.ap = ap
        self.n = n
        self.off = 0
        self.peak = 0
        self.space = Space(n)

    def alloc(self, *shape):
        sz = int(np.prod(shape))
        sz_al = (sz + 15) // 16 * 16
        assert self.off + sz_al <= self.n, f"arena overflow {self.off}+{sz_al}>{self.n}"
        v = self.ap[:, self.off:self.off + sz]
        a0 = self.off
        self.space.split(a0)
        self.space.split(a0 + sz_al)
        self.off += sz_al
        self.peak = max(self.peak, self.off)
        if len(shape) == 2:
            v = v.rearrange("p (a b) -> p a b", a=shape[0])
        elif len(shape) == 3:
            v = v.rearrange("p (a b c) -> p a b c", a=shape[0], b=shape[1])
        elif len(shape) == 4:
            v = v.rearrange("p (a b c d) -> p a b c d", a=shape[0], b=shape[1], c=shape[2])
        return Buf(v, self.space, a0, a0 + sz_al)


def build():
    nc = bass.Bass("TRN2", target_bir_lowering=False, dynamic_dma_scratch_size=2048)
    P = Prog()

    def din(name, shape):
        return nc.dram_tensor(name, list(shape), F32, kind="ExternalInput").ap()

    def dout(name, shape):
        return nc.dram_tensor(name, list(shape), F32, kind="ExternalOutput").ap()

    xT = din("xT", [D, TP]); xsT = din("xsT", [D, NS]); memT = din("memT", [D, NMEM])
    kcT = din("kcT", [DEPTH, NS, 4, 256, 256]); vc = din("vc", [DEPTH, NS, 256, D])
    s5r = din("s5r", [2, 128, 16, NS]); s5i = din("s5i", [2, 128, 16, NS])
    hgs = din("hgs", [2, NS, 4, 128, 128])
    cfs = din("cfs", [2, 128, 8, NS, 30]); ffs = din("ffs", [DEPTH, 128, FC, NS, 2])
    gains = din("gains", [128, DEPTH, 7, 8])
    w_ab_in = din("w_ab_in", [2, D, 2560]); w_ab_out = din("w_ab_out", [2, D, D])
    w_glu = din("w_glu", [2, 512, 512])
    w_conf_in = din("w_conf_in", [2, D, 2048]); w_conf_out = din("w_conf_out", [2, D, D])
    w_xq = din("w_xq", [DEPTH, D, D]); w_xk = din("w_xk", [DEPTH, D, D])
    w_xv = din("w_xv", [DEPTH, D, D]); w_xo = din("w_xo", [DEPTH, D, D])
    w_ffn_in = din("w_ffn_in", [DEPTH, D, 2 * DFF]); w_ffn_out = din("w_ffn_out", [DEPTH, DFF, D])
    lamr_p = din("lamr_p", [2, 128, 16]); lami_p = din("lami_p", [2, 128, 16]); ldt_p = din("ldt_p", [2, 128, 16])
    brN = din("brN", [2, 128, 16, 128]); biN = din("biN", [2, 128, 16, 128])
    crN = din("crN", [2, 128, 16, 128]); ciN = din("ciN", [2, 128, 16, 128])
    s5d = din("s5d", [128, 2, 4]); bglu = din("bglu", [128, 2, 4])
    lbl_p = din("lbl_p", [128, 2, 4]); lbl_b = din("lbl_b", [2, 512]); gnp = din("gnp", [128, 2, 4])
    cdw = din("cdw", [128, 2, 8, 31]); cdb = din("cdb", [128, 2, 8]); clg = din("clg", [128, 2, 8]); clb = din("clb", [128, 2, 8])
    fdw = din("fdw", [128, DEPTH, FC, 3]); fdb = din("fdb", [128, DEPTH, FC])
    c_ident = din("c_ident", [128, 128]); c_mask = din("c_mask", [128, 128]); c_cmask = din("c_cmask", [128, 4])

    yT = dout("yT", [D, TP]); ysT = dout("ysT", [D, NS])
    o_ps5r = dout("o_ps5r", [2, 128, 16]); o_ps5i = dout("o_ps5i", [2, 128, 16])
    o_phg = dout("o_phg", [2, 4, 128, 128])
    o_pconf = dout("o_pconf", [2, 128, 8, 30]); o_pffn = dout("o_pffn", [DEPTH, 128, FC, 2])
    o_pmk = dout("o_pmk", [DEPTH, NMEM, D]); o_pmv = dout("o_pmv", [DEPTH, NMEM, D])
    o_ss5r = dout("o_ss5r", [2, 128, 16, NS]); o_ss5i = dout("o_ss5i", [2, 128, 16, NS])
    o_shg = dout("o_shg", [2, NS, 4, 128, 128])
    o_sconf = dout("o_sconf", [2, 128, 8, NS, 30]); o_sffn = dout("o_sffn", [DEPTH, 128, FC, NS, 2])

    NF = NF_
    NB = NB_
    ctx = []

    def enter(c):
        ctx.append(c)
        return c.__enter__()

    af_t = enter(nc.sbuf_tensor("af", [128, NF], F32))
    ab_t = enter(nc.sbuf_tensor("ab", [128, NB], BF16))
    FA = Arena(af_t, NF)
    BA = Arena(ab_t, NB)
    psb = [Buf(enter(nc.psum_tensor(f"ps{i}", [128, 512], F32))[:, :]) for i in range(7)]
    psbf = Buf(enter(nc.psum_tensor("psbf", [128, 1024], BF16))[:, :])
    for b_ in psb + [psbf]:
        b_.excl = True
    sems = {e: enter(nc.semaphore("s_" + e)) for e in ENGS if e != 'sp'}
    dsems = [enter(nc.semaphore(f"d{i}")) for i in range(NDMASEM)]
    pscnt = [0]

    psrot = [4]

    def psum():
        b = psb[pscnt[0] % psrot[0]]
        pscnt[0] += 1
        return b

    def psfix(i):
        return psb[4 + i]

    def dma(out, in_, rd=(), wr=(), grp=None, **kw):
        P.op('sp', lambda e, o=out, i=in_: e.dma_start(out=o, in_=i, **kw), rd, wr, grp)

    def mm(out, lhsT, rhs, start, stop, rd, wr):
        P.op('pe', lambda e: e.matmul(out, lhsT=lhsT, rhs=rhs, start=start, stop=stop), rd, wr)

    def act(out, in_, func, rd, wr, bias=None, scale=None, grp=None, eng='act'):
        kw = {}
        if bias is not None:
            kw['bias'] = bias
        if scale is not None:
            kw['scale'] = scale
        P.op('act', lambda e: e.activation(out=out, in_=in_, func=func, **kw), rd, wr, grp)

    def tt(eng, out, in0, in1, op, rd, wr, grp=None):
        P.op(eng, lambda e: e.tensor_tensor(out=out, in0=in0, in1=in1, op=op), rd, wr, grp)

    def ts(eng, out, in0, s1, op0, rd, wr, s2=None, op1=None, grp=None):
        if op1 is None:
            P.op(eng, lambda e: e.tensor_scalar(out=out, in0=in0, scalar1=s1, scalar2=None, op0=op0), rd, wr, grp)
        else:
            P.op(eng, lambda e: e.tensor_scalar(out=out, in0=in0, scalar1=s1, scalar2=s2, op0=op0, op1=op1), rd, wr, grp)

    def stt(out, in0, scalar, in1, op0, op1, rd, wr, grp=None):
        P.op('dve', lambda e: e.scalar_tensor_tensor(out=out, in0=in0, scalar=scalar, in1=in1, op0=op0, op1=op1), rd, wr, grp)

    def cp(eng, out, in_, rd, wr, grp=None):
        if eng == 'act':
            P.op('act', lambda e: e.activation(out=out, in_=in_, func=AF.Copy), rd, wr, grp)
        else:
            P.op(eng, lambda e: e.tensor_copy(out=out, in_=in_), rd, wr, grp)

    def memset(eng, buf, ap, val):
        P.op(eng, lambda e: e.memset(ap, val), (), [buf])

    def bc(ap2d, n):
        return ap2d.unsqueeze(1).to_broadcast([128, n, ap2d.shape[1]])

    x = FA.alloc(KC, W)
    gn_all = FA.alloc(DEPTH, 7, 8)
    ident = FA.alloc(128)
    epsb = FA.alloc(1)
    t_s5d = FA.alloc(2, 4); t_bglu = FA.alloc(2, 4); t_lbl = FA.alloc(2, 4); t_gnp = FA.alloc(2, 4)
    t_cdw = FA.alloc(2, 8, 31); t_cdb = FA.alloc(2, 8); t_clg = FA.alloc(2, 8); t_clb = FA.alloc(2, 8)
    t_fdw = FA.alloc(DEPTH, FC, 3); t_fdb = FA.alloc(DEPTH, FC)
    lb_p = FA.alloc(2, 4); omlb_p = FA.alloc(2, 4)
    hg_S = [[FA.alloc(128) for _h in range(4)] for _ in range(2)]
    s5car = [(FA.alloc(16), FA.alloc(16)) for _ in range(2)]
    ffn_hist = [FA.alloc(FC, 2) for _ in range(DEPTH)]
    mask16 = FA.alloc(128)
    cmask = FA.alloc(4)
    ones_bf = BA.alloc(128)
    ident_bf = BA.alloc(128)
    conf_hist = [BA.alloc(8, 30) for _ in range(2)]
    mhat = BA.alloc(KC, NMEM)
    stage = [FA.alloc(1024) for _ in range(NSTAGE)]
    stcnt = [0]

    for (t, src) in [(gn_all, gains), (t_s5d, s5d), (t_bglu, bglu), (t_lbl, lbl_p), (t_gnp, gnp), (t_cdw, cdw),
                     (t_cdb, cdb), (t_clg, clg), (t_clb, clb), (t_fdw, fdw), (t_fdb, fdb)]:
        dma(t.ap, src, wr=[t])
    dma(ident.ap, c_ident, wr=[ident])
    dma(mask16.ap, c_mask, wr=[mask16])
    dma(cmask.ap, c_cmask, wr=[cmask])
    cp('dve', ident_bf.ap, ident.ap, [ident], [ident_bf])
    memset('dve', ones_bf, ones_bf.ap, 1.0)
    memset('dve', epsb, epsb.ap, EPS)
    memset('dve', lb_p, lb_p.ap[:, 0, :], 0.0)
    tt('dve', lb_p.ap[:, 1, :], t_lbl.ap[:, 1, :], t_lbl.ap[:, 0, :], ALU.subtract, [t_lbl], [lb_p])
    act(lb_p.ap[:, 1, :], lb_p.ap[:, 1, :], AF.Sigmoid, [lb_p], [lb_p])
    ts('dve', omlb_p.ap, lb_p.ap, -1.0, ALU.mult, [lb_p], [omlb_p], s2=1.0, op1=ALU.add)
    for j in range(2):
        for _h in range(4):
            memset('dve', hg_S[j][_h], hg_S[j][_h].ap, 0.0)
        memset('dve', s5car[j][0], s5car[j][0].ap, 0.0)
        memset('dve', s5car[j][1], s5car[j][1].ap, 0.0)
        memset('pool', conf_hist[j], conf_hist[j].ap, 0.0)
    for li in range(DEPTH):
        memset('dve', ffn_hist[li], ffn_hist[li].ap, 0.0)
    fmark = FA.off
    bmark = BA.off

    cast_rr = [0]

    def load_w(dst, dview, src, kc_n, ncols, gain=None):
        if ncols <= 128:
            st = stage[stcnt[0] % NSTAGE]
            stcnt[0] += 1
            sv = st.ap[:, 0:kc_n * ncols].rearrange("p (a b) -> p a b", a=kc_n)
            dma(sv, src.rearrange("(k p) c -> p k c", p=128), wr=[st])
            eng = ['act', 'dve'][cast_rr[0] % 2]
            cast_rr[0] += 1
            cp(eng, dview, sv, [st], [dst], grp='ldw')
            return
        for k in range(kc_n):
            for c0 in range(0, ncols, 1024):
                cw = min(1024, ncols - c0)
                st = stage[stcnt[0] % NSTAGE]
                stcnt[0] += 1
                dma(st.ap[:, 0:cw], src[k * 128:(k + 1) * 128, c0:c0 + cw], wr=[st])
                eng = ['act', 'dve'][cast_rr[0] % 2]
                cast_rr[0] += 1
                cp(eng, dview[:, k, c0:c0 + cw], st.ap[:, 0:cw], [st], [dst], grp='ldw')

    def subtiles(s):
        r = [(0, 512, False), (512, 512, False)]
        if s == 1:
            r.append((STK, NS, True))
        return r

    def rstd_of(src_fn, nch, c0, n, inv_dim, rdbufs, src_all=None):
        sq = BA.alloc(nch, 512)
        if src_all is not None:
            act(sq.ap[:, :, 0:n], src_all, AF.Square, rdbufs, [sq])
        else:
            for c in range(nch):
                act(sq.ap[:, c, 0:n], src_fn(c), AF.Square, rdbufs, [sq], grp='sq')
        ps = psum()
        for c in range(nch):
            mm(ps.ap[:, 0:n], ones_bf.ap, sq.ap[:, c, 0:n], c == 0, c == nch - 1, [ones_bf, sq], [ps])
        rs = FA.alloc(512)
        act(rs.ap[:, 0:n], ps.ap[:, 0:n], AF.Ln, [ps, epsb], [rs], bias=epsb.ap[:, 0:1], scale=inv_dim)
        act(rs.ap[:, 0:n], rs.ap[:, 0:n], AF.Exp, [rs], [rs], scale=-0.5)
        return rs

    def prenorm(s, hb, gain):
        for (c0, n, smp) in subtiles(s):
            fm, bm = FA.off, BA.off
            rs = rstd_of(lambda c: x.ap[:, c, c0:c0 + n], KC, c0, n, 1.0 / D, [x], src_all=x.ap[:, :, c0:c0 + n])
            for c in range(KC):
                stt(hb.ap[:, c, c0:c0 + n], x.ap[:, c, c0:c0 + n], gain[:, c:c + 1], rs.ap[:, 0:n], ALU.mult, ALU.mult,
                    [x, rs, gn_all], [hb], grp=('pn', c0))
            FA.off, BA.off = fm, bm

    def postnorm(s, yb, li, ng):
        for (c0, n, smp) in subtiles(s):
            fm, bm = FA.off, BA.off
            rs = rstd_of(lambda c: yb.ap[:, c, c0:c0 + n], KC, c0, n, 1.0 / D, [yb], src_all=yb.ap[:, :, c0:c0 + n])
            for c in range(KC):
                stt(yb.ap[:, c, c0:c0 + n], yb.ap[:, c, c0:c0 + n], gn_all.ap[:, li, ng, c:c + 1], rs.ap[:, 0:n],
                    ALU.mult, ALU.mult, [yb, gn_all, rs], [yb], grp=('po', c0))
            tt('dve', x.ap[:, :, c0:c0 + n], x.ap[:, :, c0:c0 + n], yb.ap[:, :, c0:c0 + n], ALU.add, [x, yb], [x])
            FA.off, BA.off = fm, bm

    def linear(wb, wview, col0, mchunks, kch, rhs_fn, rdbufs, n, evac):
        for m in range(mchunks):
            ps = psum()
            for k in range(kch):
                mm(ps.ap[:, 0:n], wview[:, k, col0 + m * 128: col0 + (m + 1) * 128], rhs_fn(k), k == 0, k == kch - 1,
                   [wb] + rdbufs, [ps])
            evac(m, ps)

    def gelu(out, in_, rd, wr, grp=None):
        act(out, in_, AF.Gelu_apprx_tanh, rd, wr, grp=grp)

    nsub = [0]

    def done():
        nsub[0] += 1
        return STOP_AFTER is not None and nsub[0] >= STOP_AFTER

    def reset():
        P.barrier()
        FA.off, BA.off = fmark, bmark

    def mem_prep():
        mt = FA.alloc(KC, NMEM)
        dma(mt.ap, memT.rearrange("(c p) m -> p c m", p=128), wr=[mt])
        rs = rstd_of(lambda c: mt.ap[:, c, :], KC, 0, NMEM, 1.0 / D, [mt])
        tt('dve', mhat.ap, mt.ap, bc(rs.ap[:, 0:NMEM], KC), ALU.mult, [mt, rs], [mhat])
        reset()

    def xattn(s, li):
        psrot[0] = 7
        gq = gn_all.ap[:, li, 2, :]
        gm = gn_all.ap[:, li, 6, :]
        kfm = BA.alloc(KC, NMEM)
        vtm = BA.alloc(2, D)
        mg = BA.alloc(KC, NMEM)
        for c in range(KC):
            ts('dve', mg.ap[:, c, :], mhat.ap[:, c, :], gm[:, c:c + 1], ALU.mult, [mhat, gn_all], [mg], grp='mg')
        wk = BA.alloc(KC, D)
        load_w(wk, wk.ap, w_xk[li], KC, D)
        otm = FA.alloc(2, D)
        for mb in range(2):
            for half in range(2):
                ps = psum()
                for k in range(KC):
                    mm(ps.ap, mg.ap[:, k, mb * 128:(mb + 1) * 128], wk.ap[:, k, half * 512:(half + 1) * 512], k == 0, k == KC - 1, [mg, wk], [ps])
                cp('act', otm.ap[:, mb, half * 512:(half + 1) * 512], ps.ap, [ps], [otm], grp='otm')
        dma(o_pmk[li].rearrange("(b p) d -> p b d", p=128), otm.ap, rd=[otm])
        linear(wk, wk.ap, 0, KC, KC, lambda k: mg.ap[:, k, :], [mg], NMEM,
               lambda m, ps: cp('dve', kfm.ap[:, m, :], ps.ap[:, 0:NMEM], [ps], [kfm], grp='kfm'))
        wv = BA.alloc(KC, D)
        load_w(wv, wv.ap, w_xv[li], KC, D)
        otv = FA.alloc(2, D)
        for mb in range(2):
            for half in range(2):
                ps = psum()
                for k in range(KC):
                    mm(ps.ap, mg.ap[:, k, mb * 128:(mb + 1) * 128], wv.ap[:, k, half * 512:(half + 1) * 512], k == 0, k == KC - 1, [mg, wv], [ps])
                cp('act', otv.ap[:, mb, half * 512:(half + 1) * 512], ps.ap, [ps], [otv], grp='otv')
                cp('dve', vtm.ap[:, mb, half * 512:(half + 1) * 512], ps.ap, [ps], [vtm], grp='vtm')
        dma(o_pmv[li].rearrange("(b p) d -> p b d", p=128), otv.ap, rd=[otv])
        P.barrier()
        FA.off = fmark
        BA.off = bmark + KC * NMEM + 2 * D
        hb = BA.alloc(KC, W)
        wq = BA.alloc(KC, D)
        load_w(wq, wq.ap, w_xq[li], KC, D)
        wo = BA.alloc(KC, D)
        load_w(wo, wo.ap, w_xo[li], KC, D)
        prenorm(s, hb, gq)
        usets = [dict(mx=FA.alloc(4), pe=FA.alloc(4, NMEM), sm=FA.alloc(4), pb=BA.alloc(4, NMEM), ptb=BA.alloc(8, 128)) for _ in range(2)]
        ucnt = 0
        yb = FA.alloc(KC, W)
        scl = 1.0 / 16.0
        for (c0, n, smp) in subtiles(s):
            fm, bm = FA.off, BA.off
            qb = BA.alloc(KC, 512)
            ob = BA.alloc(KC, 512)
            if not smp:
                linear(wq, wq.ap, 0, KC, KC, lambda k: hb.ap[:, k, c0:c0 + n], [hb], n,
                       lambda m, ps: cp('act', qb.ap[:, m, 0:n], ps.ap[:, 0:n], [ps], [qb], grp='qb'))
                def scores(blk):
                    pss_ = [psum(), psum()]
                    for h in range(4):
                        ps = pss_[h // 2]
                        off = (h % 2) * NMEM
                        for dc in range(2):
                            mm(ps.ap[:, off:off + NMEM], qb.ap[:, 2 * h + dc, blk * 128:(blk + 1) * 128], kfm.ap[:, 2 * h + dc, :], dc == 0, dc == 1, [qb, kfm], [ps])
                    return pss_
                nblk = n // 128
                sc = scores(0)
                for blk in range(nblk):
                    nxt = scores(blk + 1) if blk + 1 < nblk else None
                    us = usets[ucnt % 2]
                    ucnt += 1
                    mx4, sm4, pe4, pb4, ptb4 = us['mx'], us['sm'], us['pe'], us['pb'], us['ptb']
                    for q in range(2):
                        P.op('dve', lambda e, o=mx4.ap[:, 2 * q:2 * q + 2], i=sc[q].ap.rearrange("p (a b) -> p a b", a=2): e.reduce_max(out=o, in_=i, axis=AX.X), [sc[q]], [mx4], 'mx')
                    ts('dve', mx4.ap, mx4.ap, -scl, ALU.mult, [mx4], [mx4])
                    for h in range(4):
                        P.op('act', lambda e, o=pe4.ap[:, h, :], i=sc[h // 2].ap[:, (h % 2) * NMEM:(h % 2 + 1) * NMEM], b=mx4.ap[:, h:h + 1], a=sm4.ap[:, h:h + 1]:
                             e.activation(out=o, in_=i, func=AF.Exp, bias=b, scale=scl, accum_out=a), [sc[h // 2], mx4], [pe4, sm4], 'ex')
                    P.op('dve', lambda e, o=sm4.ap, i=sm4.ap: e.reciprocal(out=o, in_=i), [sm4], [sm4])
                    tt('dve', pb4.ap, pe4.ap, sm4.ap.unsqueeze(2).to_broadcast([128, 4, NMEM]), ALU.mult, [pe4, sm4], [pb4])
                    for h in range(4):
                        for mb in range(2):
                            P.op('pe', lambda e, o=psbf.ap[:, (h * 2 + mb) * 128:(h * 2 + mb + 1) * 128], i=pb4.ap[:, h, mb * 128:(mb + 1) * 128]:
                                 e.transpose(out=o, in_=i, identity=ident_bf.ap), [pb4, ident_bf], [psbf])
                    cp('act', ptb4.ap, psbf.ap.rearrange("p (a b) -> p a b", a=8), [psbf], [ptb4])
                    for q in range(2):
                        ps2 = psum()
                        for r in range(4):
                            hd = 4 * q + r
                            h = hd // 2
                            for mb in range(2):
                                mm(ps2.ap[:, r * 128:(r + 1) * 128], vtm.ap[:, mb, hd * 128:(hd + 1) * 128], ptb4.ap[:, h * 2 + mb, :], mb == 0, mb == 1, [vtm, ptb4], [ps2])
                        cp('dve', ob.ap[:, 4 * q:4 * q + 4, blk * 128:(blk + 1) * 128], ps2.ap.rearrange("p (a b) -> p a b", a=4), [ps2], [ob], grp='ob')
                    sc = nxt
            else:
                psrot[0] = 4
                qf = FA.alloc(KC, NS)
                linear(wq, wq.ap, 0, KC, KC, lambda k: hb.ap[:, k, c0:c0 + n], [hb], n,
                       lambda m, ps: cp('act', qf.ap[:, m, :], ps.ap[:, 0:n], [ps], [qf], grp='qf'))
                scT = FA.alloc(2, 64)
                kts = [FA.alloc(KC * NMEM) for _ in range(2)]
                vts = kts
                pss = psfix(0)
                for nn in range(NS):
                    kt = kts[nn % 2]
                    ktv = kt.ap.rearrange("p (a b) -> p a b", a=KC)
                    dma(ktv, kcT[li, nn].rearrange("h (c p) m -> p (h c) m", p=128), wr=[kt])
                    for h in range(4):
                        for mb in range(2):
                            col = mb * 64 + nn * 4 + h
                            for dc in range(2):
                                mm(pss.ap[:, col:col + 1], ktv[:, 2 * h + dc, mb * 128:(mb + 1) * 128], qf.ap[:, 2 * h + dc, nn:nn + 1], dc == 0, dc == 1, [kt, qf], [pss])
                cp('dve', scT.ap, pss.ap[:, 0:128].rearrange("p (a b) -> p a b", a=2), [pss], [scT])
                pst = psum()
                for mb in range(2):
                    P.op('pe', lambda e, o=pst.ap[0:64, mb * 128:(mb + 1) * 128], i=scT.ap[:, mb, :]:
                         e.transpose(out=o, in_=i, identity=ident.ap), [scT, ident], [pst])
                mx = FA.alloc(1)
                P.op('dve', lambda e, o=mx.ap[0:64, :], i=pst.ap[0:64, 0:NMEM]: e.reduce_max(out=o, in_=i, axis=AX.X), [pst], [mx])
                ts('dve', mx.ap[0:64, :], mx.ap[0:64, :], -scl, ALU.mult, [mx], [mx])
                pe_ = FA.alloc(NMEM)
                sm = FA.alloc(1)
                P.op('act', lambda e, o=pe_.ap[0:64, :], i=pst.ap[0:64, 0:NMEM], b=mx.ap[0:64, 0:1], a=sm.ap[0:64, 0:1]:
                     e.activation(out=o, in_=i, func=AF.Exp, bias=b, scale=scl, accum_out=a), [pst, mx], [pe_, sm])
                P.op('dve', lambda e, o=sm.ap[0:64, :], i=sm.ap[0:64, :]: e.reciprocal(out=o, in_=i), [sm], [sm])
                ts('dve', pe_.ap[0:64, :], pe_.ap[0:64, :], sm.ap[0:64, 0:1], ALU.mult, [pe_, sm], [pe_])
                pT = FA.alloc(2, 64)
                pst2 = psum()
                for mb in range(2):
                    P.op('pe', lambda e, o=pst2.ap[:, mb * 64:(mb + 1) * 64], i=pe_.ap[0:64, mb * 128:(mb + 1) * 128]:
                         e.transpose(out=o, in_=i, identity=ident.ap[0:64, 0:64]), [pe_, ident], [pst2])
                cp('dve', pT.ap, pst2.ap[:, 0:128].rearrange("p (a b) -> p a b", a=2), [pst2], [pT])
                pso = psfix(1)
                for nn in range(NS):
                    vt = vts[nn % 2]
                    vtv = vt.ap.rearrange("p (a b) -> p a b", a=2)
                    dma(vtv, vc[li, nn].rearrange("(b p) d -> p b d", p=128), wr=[vt])
                    for h in range(4):
                        for dc in range(2):
                            col = (2 * h + dc) * NS + nn
                            for mb in range(2):
                                mm(pso.ap[:, col:col + 1], vtv[:, mb, (2 * h + dc) * 128:(2 * h + dc + 1) * 128],
                                   pT.ap[:, mb, nn * 4 + h:nn * 4 + h + 1], mb == 0, mb == 1, [vt, pT], [pso])
                cp('dve', ob.ap[:, :, 0:NS], pso.ap[:, 0:KC * NS].rearrange("p (a b) -> p a b", a=KC), [pso], [ob])
            linear(wo, wo.ap, 0, KC, KC, lambda k: ob.ap[:, k, 0:n], [ob], n,
                   lambda m, ps: cp('act', yb.ap[:, m, c0:c0 + n], ps.ap[:, 0:n], [ps], [yb], grp=('yb', c0)))
            FA.off, BA.off = fm, bm
        postnorm(s, yb, li, 3)
        psrot[0] = 4
        reset()

    def ffn(s, li):
        psrot[0] = 7
        hb = BA.alloc(KC, W)
        gf = gn_all.ap[:, li, 4, :]
        yb = FA.alloc(KC, W)
        win = w_ffn_in[li]
        wout = w_ffn_out[li]
        HALF = FC // 2
        gb = BA.alloc(HALF, W)
        wabs = [BA.alloc(KC, 256) for _ in range(4)]
        wo = BA.alloc(HALF, D)
        asb = [FA.alloc(2 + 512) for _ in range(2)]
        tb = [FA.alloc(512) for _ in range(2)]
        sth = FA.alloc(FC, NS, 2)
        sout = FA.alloc(FC, NS, 2)
        pout = FA.alloc(FC, 2)
        if s == 1:
            dma(sth.ap, ffs[li], wr=[sth])
            cp('pool', sout.ap[:, :, :, 0], sth.ap[:, :, :, 1], [sth], [sout])
        hist = ffn_hist[li]
        it = 0

        def ldchunk(c):
            wab = wabs[c % 4]
            load_w(wab, wab.ap[:, :, 0:128], win[:, c * 128:(c + 1) * 128], KC, 128)
            load_w(wab, wab.ap[:, :, 128:256], win[:, DFF + c * 128:DFF + (c + 1) * 128], KC, 128)
        ldchunk(0)
        ldchunk(1)
        prenorm(s, hb, gf)
        for half in range(2):
            for cl in range(HALF):
                c = half * HALF + cl
                wab = wabs[c % 4]
                if c + 2 < FC:
                    ldchunk(c + 2)
                if cl == 2:
                    load_w(wo, wo.ap, wout[half * HALF * 128:(half + 1) * HALF * 128, :], HALF, D)
                for (c0, n, smp) in subtiles(s):
                    a_ = asb[it % 2]
                    t_ = tb[it % 2]
                    it += 1
                    psa = psum()
                    psb_ = psum()
                    for k in range(KC):
                        mm(psa.ap[:, 0:n], wab.ap[:, k, 0:128], hb.ap[:, k, c0:c0 + n], k == 0, k == KC - 1, [wab, hb], [psa])
                    for k in range(KC):
                        mm(psb_.ap[:, 0:n], wab.ap[:, k, 128:256], hb.ap[:, k, c0:c0 + n], k == 0, k == KC - 1, [wab, hb], [psb_])
                    w0 = t_fdw.ap[:, li, c, 0:1]
                    w1 = t_fdw.ap[:, li, c, 1:2]
                    w2 = t_fdw.ap[:, li, c, 2:3]
                    bb = t_fdb.ap[:, li, c:c + 1]
                    if not smp:
                        cp('pool', a_.ap[:, 0:2], hist.ap[:, c, :], [hist], [a_])
                        cp('act', a_.ap[:, 2:2 + n], psa.ap[:, 0:n], [psa], [a_], grp='a')
                        cp('pool', hist.ap[:, c, :], a_.ap[:, n:n + 2], [a_], [hist])
                        P.op('act', lambda e, o=t_.ap[:, 0:n], i=a_.ap[:, 2:2 + n], sc=w2, b=bb: e.activation(out=o, in_=i, func=AF.Identity, bias=b, scale=sc),
                             [a_, t_fdw, t_fdb], [t_])
                        stt(t_.ap[:, 0:n], a_.ap[:, 1:1 + n], w1, t_.ap[:, 0:n], ALU.mult, ALU.add, [a_, t_, t_fdw], [t_])
                        stt(t_.ap[:, 0:n], a_.ap[:, 0:n], w0, t_.ap[:, 0:n], ALU.mult, ALU.add, [a_, t_, t_fdw], [t_])
                        if s == 1 and c0 == 512:
                            cp('pool', pout.ap[:, c, :], a_.ap[:, n:n + 2], [a_], [pout], grp='po')
                    else:
                        cp('act', sout.ap[:, c, :, 1], psa.ap[:, 0:n], [psa], [sout], grp='so')
                        P.op('act', lambda e, o=t_.ap[:, 0:n], i=psa.ap[:, 0:n], sc=w2, b=bb: e.activation(out=o, in_=i, func=AF.Identity, bias=b, scale=sc),
                             [psa, t_fdw, t_fdb], [t_])
                        stt(t_.ap[:, 0:n], sth.ap[:, c, :, 1], w1, t_.ap[:, 0:n], ALU.mult, ALU.add, [sth, t_, t_fdw], [t_])
                        stt(t_.ap[:, 0:n], sth.ap[:, c, :, 0], w0, t_.ap[:, 0:n], ALU.mult, ALU.add, [sth, t_, t_fdw], [t_])
                    gelu(t_.ap[:, 0:n], t_.ap[:, 0:n], [t_], [t_])
                    tt('dve', gb.ap[:, cl, c0:c0 + n], t_.ap[:, 0:n], psb_.ap[:, 0:n], ALU.mult, [t_, psb_], [gb], grp=('gb', half))
            for m in range(KC):
                for (c0, n, smp) in subtiles(s):
                    ps = psum()
                    for k in range(HALF):
                        mm(ps.ap[:, 0:n], wo.ap[:, k, m * 128:(m + 1) * 128], gb.ap[:, k, c0:c0 + n], k == 0, k == HALF - 1, [wo, gb], [ps])
                    if half == 0:
                        cp('act', yb.ap[:, m, c0:c0 + n], ps.ap[:, 0:n], [ps], [yb], grp=('y0', c0))
                    else:
                        tt('dve', yb.ap[:, m, c0:c0 + n], yb.ap[:, m, c0:c0 + n], ps.ap[:, 0:n], ALU.add, [yb, ps], [yb], grp=('y1', c0))
        if s == 1:
            dma(o_pffn[li], pout.ap, rd=[pout])
            dma(o_sffn[li], sout.ap, rd=[sout])
        postnorm(s, yb, li, 5)
        psrot[0] = 4
        reset()

    def conf(s, li):
        j = li // 2
        psrot[0] = 7
        hb = BA.alloc(KC, W)
        gm = gn_all.ap[:, li, 0, :]
        ub = BA.alloc(KC, 30 + W)
        cbuf = FA.alloc(KC, W)
        fa_keep = FA.off
        hist = conf_hist[j]
        cp('dve', ub.ap[:, :, 0:30], hist.ap, [hist], [ub])
        win = BA.alloc(KC, 2048)
        load_w(win, win.ap, w_conf_in[j], KC, 2048)
        prenorm(s, hb, gm)
        for (c0, n, smp) in subtiles(s):
            fm, bm = FA.off, BA.off
            sg = FA.alloc(512)
            for m in range(KC):
                psa = psum()
                psg = psum()
                for k in range(KC):
                    mm(psa.ap[:, 0:n], win.ap[:, k, m * 128:(m + 1) * 128], hb.ap[:, k, c0:c0 + n], k == 0, k == KC - 1, [win, hb], [psa])
                for k in range(KC):
                    mm(psg.ap[:, 0:n], win.ap[:, k, D + m * 128:D + (m + 1) * 128], hb.ap[:, k, c0:c0 + n], k == 0, k == KC - 1, [win, hb], [psg])
                act(sg.ap[:, 0:n], psg.ap[:, 0:n], AF.Sigmoid, [psg], [sg])
                tt('dve', ub.ap[:, m, 30 + c0:30 + c0 + n], sg.ap[:, 0:n], psa.ap[:, 0:n], ALU.mult, [sg, psa], [ub], grp=('ub', c0))
            FA.off, BA.off = fm, bm
        cp('dve', hist.ap, ub.ap[:, :, STK:STK + 30], [ub], [hist])
        P.barrier()
        BA.off -= KC * 2048
        dgs = [BA.alloc(31, 128) for _ in range(2)]

        def build_dg(m):
            dg = dgs[m % 2]
            for k in range(31):
                if k % 2 == 0:
                    P.op('act', lambda e, o=dg.ap[:, k, :], sc=t_cdw.ap[:, j, m, k:k + 1]: e.activation(out=o, in_=ident_bf.ap, func=AF.Copy, scale=sc),
                         [ident_bf, t_cdw], [dg], 'dg')
                else:
                    ts('dve', dg.ap[:, k, :], ident_bf.ap, t_cdw.ap[:, j, m, k:k + 1], ALU.mult, [ident_bf, t_cdw], [dg], grp='dg')
        build_dg(0)
        for m in range(KC):
            dg = dgs[m % 2]
            if m + 1 < KC:
                build_dg(m + 1)
            for (c0, n, smp) in subtiles(s):
                if smp:
                    continue
                ps = psum()
                for k in range(31):
                    mm(ps.ap[:, 0:n], dg.ap[:, k, :], ub.ap[:, m, c0 + k:c0 + k + n], k == 0, k == 30, [dg, ub], [ps])
                P.op('act', lambda e, o=cbuf.ap[:, m, c0:c0 + n], i=ps.ap[:, 0:n], b=t_cdb.ap[:, j, m:m + 1]: e.activation(out=o, in_=i, func=AF.Identity, bias=b),
                     [ps, t_cdb], [cbuf], ('cb', c0))
        if s == 1:
            ext = FA.alloc(KC, NS, 31)
            dma(ext.ap[:, :, :, 0:30], cfs[j], wr=[ext])
            cp('dve', ext.ap[:, :, :, 30], ub.ap[:, :, 30 + STK:30 + STK + NS], [ub], [ext])
            dma(o_sconf[j], ext.ap[:, :, :, 1:31], rd=[ext])
            pr = FA.alloc(NS, 31)
            cs = FA.alloc(KC, NS)
            for m in range(KC):
                tt('dve', pr.ap, ext.ap[:, m], t_cdw.ap[:, j, m, :].unsqueeze(1).to_broadcast([128, NS, 31]), ALU.mult, [ext, t_cdw], [pr])
                P.op('dve', lambda e, m=m: e.reduce_sum(out=cs.ap[:, m, :], in_=pr.ap, axis=AX.X), [pr], [cs], 'cs')
            for m in range(KC):
                ts('dve', cbuf.ap[:, m, STK:STK + NS], cs.ap[:, m, :], t_cdb.ap[:, j, m:m + 1], ALU.add, [cs, t_cdb], [cbuf], grp='cbs')
            pc = FA.alloc(KC, 30)
            cp('dve', pc.ap, ub.ap[:, :, STK:STK + 30], [ub], [pc])
            dma(o_pconf[j], pc.ap, rd=[pc])
        P.barrier()
        BA.off = bmark
        FA.off = fa_keep
        wo = BA.alloc(KC, D)
        load_w(wo, wo.ap, w_conf_out[j], KC, D)
        yb = cbuf
        for (c0, n, smp) in subtiles(s):
            fm, bm = FA.off, BA.off
            cbb = BA.alloc(KC, 512)
            cp('act', cbb.ap[:, :, 0:n], cbuf.ap[:, :, c0:c0 + n], [cbuf], [cbb])
            psm = psum()
            for c in range(KC):
                mm(psm.ap[:, 0:n], ones_bf.ap, cbb.ap[:, c, 0:n], c == 0, c == KC - 1, [ones_bf, cbb], [psm])
            mu = FA.alloc(512)
            act(mu.ap[:, 0:n], psm.ap[:, 0:n], AF.Copy, [psm], [mu], scale=1.0 / D)
            xc = FA.alloc(KC, 512)
            tt('dve', xc.ap[:, :, 0:n], cbuf.ap[:, :, c0:c0 + n], bc(mu.ap[:, 0:n], KC), ALU.subtract, [cbuf, mu], [xc])
            rs = rstd_of(lambda c: xc.ap[:, c, 0:n], KC, 0, n, 1.0 / D, [xc])
            tt('dve', xc.ap[:, :, 0:n], xc.ap[:, :, 0:n], bc(rs.ap[:, 0:n], KC), ALU.mult, [xc, rs], [xc])
            sb = BA.alloc(KC, 512)
            for c in range(KC):
                P.op('act', lambda e, o=sb.ap[:, c, 0:n], i=xc.ap[:, c, 0:n], sc=t_clg.ap[:, j, c:c + 1], b=t_clb.ap[:, j, c:c + 1]:
                     e.activation(out=o, in_=i, func=AF.Silu, bias=b, scale=sc), [xc, t_clg, t_clb], [sb], 'sb')
            linear(wo, wo.ap, 0, KC, KC, lambda k: sb.ap[:, k, 0:n], [sb], n,
                   lambda m, ps: cp('act', yb.ap[:, m, c0:c0 + n], ps.ap[:, 0:n], [ps], [yb], grp=('yb', c0)))
            FA.off, BA.off = fm, bm
        postnorm(s, yb, li, 1)
        psrot[0] = 4
        reset()

    def ab(s, li):
        j = li // 2
        gm = gn_all.ap[:, li, 0, :]
        hb = BA.alloc(KC, W)
        yab = BA.alloc(KC, W)
        bm_keep = BA.off
        wu = BA.alloc(KC, 512)
        load_w(wu, wu.ap, w_ab_in[j][:, 0:512], KC, 512)
        wg = BA.alloc(4, 512)
        load_w(wg, wg.ap, w_glu[j], 4, 512)
        lr = FA.alloc(16); lim = FA.alloc(16); dt = FA.alloc(16)
        dma(lr.ap, lamr_p[j], wr=[lr]); dma(lim.ap, lami_p[j], wr=[lim]); dma(dt.ap, ldt_p[j], wr=[dt])
        act(dt.ap, dt.ap, AF.Exp, [dt], [dt])
        mag = FA.alloc(16); ang = FA.alloc(16)
        tt('dve', mag.ap, lr.ap, dt.ap, ALU.mult, [lr, dt], [mag])
        act(mag.ap, mag.ap, AF.Exp, [mag], [mag])
        tt('dve', ang.ap, lim.ap, dt.ap, ALU.mult, [lim, dt], [ang])
        NL = 10
        pwc = FA.alloc(NL, 16); pws = FA.alloc(NL, 16)
        c_ = FA.alloc(16); s_ = FA.alloc(16); t1 = FA.alloc(16); t2 = FA.alloc(16)
        hpi = FA.alloc(1)
        memset('dve', hpi, hpi.ap, math.pi / 2)
        act(s_.ap, ang.ap, AF.Sin, [ang], [s_], scale=1.0 / 32)
        act(c_.ap, ang.ap, AF.Sin, [ang, hpi], [c_], scale=1.0 / 32, bias=hpi.ap[:, 0:1])

        def dbl(co, so, ci, si):
            tt('dve', t1.ap, ci, ci, ALU.mult, [c_, pwc], [t1])
            tt('dve', t2.ap, si, si, ALU.mult, [s_, pws], [t2])
            tt('dve', so, ci, si, ALU.mult, [c_, s_, pwc, pws], [s_, pws])
            ts('dve', so, so, 2.0, ALU.mult, [s_, pws], [s_, pws])
            tt('dve', co, t1.ap, t2.ap, ALU.subtract, [t1, t2], [c_, pwc])
        for _ in range(4):
            dbl(c_.ap, s_.ap, c_.ap, s_.ap)
        dbl(pwc.ap[:, 0, :], pws.ap[:, 0, :], c_.ap, s_.ap)
        for l in range(1, NL):
            dbl(pwc.ap[:, l, :], pws.ap[:, l, :], pwc.ap[:, l - 1, :], pws.ap[:, l - 1, :])
        ar = FA.alloc(16); ai = FA.alloc(16); nai = FA.alloc(16)
        tt('dve', ar.ap, mag.ap, pwc.ap[:, 0, :], ALU.mult, [mag, pwc], [ar])
        tt('dve', ai.ap, mag.ap, pws.ap[:, 0, :], ALU.mult, [mag, pws], [ai])
        ts('dve', nai.ap, ai.ap, -1.0, ALU.mult, [ai], [nai])
        den = FA.alloc(16); er = FA.alloc(16); ei = FA.alloc(16); nei = FA.alloc(16); am1 = FA.alloc(16)
        tt('dve', den.ap, lr.ap, lr.ap, ALU.mult, [lr], [den])
        tt('dve', t1.ap, lim.ap, lim.ap, ALU.mult, [lim], [t1])
        tt('dve', den.ap, den.ap, t1.ap, ALU.add, [den, t1], [den])
        P.op('dve', lambda e: e.reciprocal(out=den.ap, in_=den.ap), [den], [den])
        ts('dve', am1.ap, ar.ap, -1.0, ALU.add, [ar], [am1])
        tt('dve', t1.ap, am1.ap, lr.ap, ALU.mult, [am1, lr], [t1])
        tt('dve', t2.ap, ai.ap, lim.ap, ALU.mult, [ai, lim], [t2])
        tt('dve', er.ap, t1.ap, t2.ap, ALU.add, [t1, t2], [er])
        tt('dve', er.ap, er.ap, den.ap, ALU.mult, [er, den], [er])
        tt('dve', t1.ap, ai.ap, lr.ap, ALU.mult, [ai, lr], [t1])
        tt('dve', t2.ap, am1.ap, lim.ap, ALU.mult, [am1, lim], [t2])
        tt('dve', ei.ap, t1.ap, t2.ap, ALU.subtract, [t1, t2], [ei])
        tt('dve', ei.ap, ei.ap, den.ap, ALU.mult, [ei, den], [ei])
        ts('dve', nei.ap, ei.ap, -1.0, ALU.mult, [ei], [nei])
        BTr = BA.alloc(16, 128); BTi = BA.alloc(16, 128); CTr = BA.alloc(16, 128); CTi = BA.alloc(16, 128)
        nCTr = BA.alloc(16, 128); nCTi = BA.alloc(16, 128)
        dgr = FA.alloc(128); dgi = FA.alloc(128); dgn = FA.alloc(128)
        bst = [FA.alloc(2, 128) for _ in range(2)]
        for cc in range(16):
            ts('dve', dgr.ap, ident.ap, er.ap[:, cc:cc + 1], ALU.mult, [ident, er], [dgr])
            ts('dve', dgi.ap, ident.ap, ei.ap[:, cc:cc + 1], ALU.mult, [ident, ei], [dgi])
            ts('dve', dgn.ap, ident.ap, nei.ap[:, cc:cc + 1], ALU.mult, [ident, nei], [dgn])
            bs = bst[cc % 2]
            dma(bs.ap[:, 0, :], brN[j, :, cc, :], wr=[bs])
            dma(bs.ap[:, 1, :], biN[j, :, cc, :], wr=[bs], grp='b2')
            ps = psum()
            mm(ps.ap[:, 0:128], bs.ap[:, 0, :], dgr.ap, True, False, [bs, dgr], [ps])
            mm(ps.ap[:, 0:128], bs.ap[:, 1, :], dgn.ap, False, True, [bs, dgn], [ps])
            mm(ps.ap[:, 128:256], bs.ap[:, 1, :], dgr.ap, True, False, [bs, dgr], [ps])
            mm(ps.ap[:, 128:256], bs.ap[:, 0, :], dgi.ap, False, True, [bs, dgi], [ps])
            cp('act', BTr.ap[:, cc, :], ps.ap[:, 0:128], [ps], [BTr], grp='bt')
            cp('act', BTi.ap[:, cc, :], ps.ap[:, 128:256], [ps], [BTi], grp='bt')
        for (dst, ndst, src) in [(CTr, nCTr, crN), (CTi, nCTi, ciN)]:
            for q4 in range(2):
                st = stage[stcnt[0] % NSTAGE]
                stcnt[0] += 1
                dma(st.ap.rearrange("p (a b) -> p a b", a=8), src[j, :, q4 * 8:(q4 + 1) * 8, :], wr=[st])
                ts('dve', ndst.ap[:, q4 * 8:(q4 + 1) * 8, :], st.ap.rearrange("p (a b) -> p a b", a=8), -1.0, ALU.mult, [st], [ndst], grp='nct')
                cp('act', dst.ap[:, q4 * 8:(q4 + 1) * 8, :], st.ap.rearrange("p (a b) -> p a b", a=8), [st], [dst], grp='ct')
        prenorm(s, hb, gm)
        uf = FA.alloc(4, W)
        ubf = BA.alloc(4, W)
        for (c0, n, smp) in subtiles(s):
            def ev(m, ps, c0=c0, n=n):
                cp('act', uf.ap[:, m, c0:c0 + n], ps.ap[:, 0:n], [ps], [uf], grp=('uf', c0))
                cp('dve', ubf.ap[:, m, c0:c0 + n], ps.ap[:, 0:n], [ps], [ubf], grp=('ubf', c0))
            linear(wu, wu.ap, 0, 4, KC, lambda k: hb.ap[:, k, c0:c0 + n], [hb], n, ev)
        x0r = FA.alloc(16, NS); x0i = FA.alloc(16, NS)
        if s == 1:
            dma(x0r.ap, s5r[j], wr=[x0r])
            dma(x0i.ap, s5i[j], wr=[x0i])
        sor = FA.alloc(16, NS); soi = FA.alloc(16, NS)
        por = FA.alloc(16); poi = FA.alloc(16)
        cosE = [FA.alloc(512) for _ in range(2)]
        sinE = [FA.alloc(512) for _ in range(2)]
        zf = uf
        zb = ubf
        carr, cari = s5car[j]
        _ta = FA.alloc(512); _tb = FA.alloc(512); _tc = FA.alloc(512); _td = FA.alloc(512)
        wk1 = [FA.alloc(512), FA.alloc(512), _ta, _tb, FA.alloc(512), FA.alloc(512)]
        wk2 = [FA.alloc(512), FA.alloc(512), _ta, _tb, FA.alloc(512), FA.alloc(512)]
        tg1 = FA.alloc(256); tg2 = FA.alloc(256)
        tcd = [[_tc, _td], [_tc, _td]]
        wk_ = [wk1, wk2]
        xb_ = [[BA.alloc(512) for _ in range(4)] for _ in range(2)]
        psy = {}
        itn = 0
        def tablegen(cc):
            ce = cosE[cc % 2]
            se = sinE[cc % 2]
            memset('dve', ce, ce.ap[:, 0:1], 1.0)
            memset('dve', se, se.ap[:, 0:1], 0.0)
            yield
            for l in range(9):
                mlen = 1 << l
                cl_ = pwc.ap[:, l, cc:cc + 1]
                sl_ = pws.ap[:, l, cc:cc + 1]
                tmp = tg1
                tmp2 = tg2
                P.op('act', lambda e, o=tmp.ap[:, 0:mlen], i=se.ap[:, 0:mlen], sc=sl_: e.activation(out=o, in_=i, func=AF.Copy, scale=sc), [se, pws], [tmp])
                P.op('act', lambda e, o=tmp2.ap[:, 0:mlen], i=ce.ap[:, 0:mlen], sc=sl_: e.activation(out=o, in_=i, func=AF.Copy, scale=sc), [ce, pws], [tmp2])
                stt(ce.ap[:, mlen:2 * mlen], ce.ap[:, 0:mlen], cl_, tmp.ap[:, 0:mlen], ALU.mult, ALU.subtract, [ce, pwc, tmp], [ce], grp=('ce', cc))
                stt(se.ap[:, mlen:2 * mlen], se.ap[:, 0:mlen], cl_, tmp2.ap[:, 0:mlen], ALU.mult, ALU.add, [se, pwc, tmp2], [se], grp=('se', cc))
                yield
        for _ in tablegen(0):
            pass
        for cc in range(16):
            uc = cc // 4
            ce = cosE[cc % 2]
            se = sinE[cc % 2]
            nxtg = tablegen(cc + 1) if cc + 1 < 16 else iter(())

            def tick():
                next(nxtg, None)
            for (c0, n, smp) in subtiles(s):
                wkk = wk_[itn % 2]
                xbb = xb_[itn % 2]
                itn += 1
                psr = psum(); psi = psum()
                mm(psr.ap[:, 0:n], BTr.ap[:, cc, :], ubf.ap[:, uc, c0:c0 + n], True, True, [BTr, ubf], [psr])
                mm(psi.ap[:, 0:n], BTi.ap[:, cc, :], ubf.ap[:, uc, c0:c0 + n], True, True, [BTi, ubf], [psi])
                p1, p2, p3, p4 = xbb
                if not smp:
                    vr, vi, ta, tb_, yr, yi = wkk
                    tc_, td_ = tcd[itn % 2]
                    tt('dve', ta.ap, psr.ap, ce.ap, ALU.mult, [psr, ce], [ta])
                    tt('dve', tb_.ap, psi.ap, se.ap, ALU.mult, [psi, se], [tb_])
                    tt('pool', vr.ap, ta.ap, tb_.ap, ALU.add, [ta, tb_], [vr])
                    tick()
                    tt('dve', tc_.ap, psi.ap, ce.ap, ALU.mult, [psi, ce], [tc_])
                    tt('dve', td_.ap, psr.ap, se.ap, ALU.mult, [psr, se], [td_])
                    tt('pool', vi.ap, tc_.ap, td_.ap, ALU.subtract, [tc_, td_], [vi])
                    tick()
                    rho = mag.ap[:, cc:cc + 1].to_broadcast([128, 512])
                    P.op('dve', lambda e, o=yr.ap, d1=vr.ap, rho=rho, ini=carr.ap[:, cc:cc + 1]: e.tensor_tensor_scan(out=o, data0=rho, data1=d1, initial=ini, op0=ALU.mult, op1=ALU.add),
                         [vr, mag, carr], [yr])
                    P.op('dve', lambda e, o=yi.ap, d1=vi.ap, rho=rho, ini=cari.ap[:, cc:cc + 1]: e.tensor_tensor_scan(out=o, data0=rho, data1=d1, initial=ini, op0=ALU.mult, op1=ALU.add),
                         [vi, mag, cari], [yi])
                    tick()
                    Rc = pwc.ap[:, 9, cc:cc + 1]
                    Rs = pws.ap[:, 9, cc:cc + 1]
                    ylr = yr.ap[:, 511:512]
                    yli = yi.ap[:, 511:512]
                    tq = FA.alloc(1)
                    if s == 1 and c0 == 512:
                        ts('dve', tq.ap, yli, se.ap[:, 511:512], ALU.mult, [yi, se], [tq])
                        stt(por.ap[:, cc:cc + 1], ylr, ce.ap[:, 511:512], tq.ap, ALU.mult, ALU.subtract, [yr, ce, tq], [por], grp='por')
                        ts('dve', tq.ap, ylr, se.ap[:, 511:512], ALU.mult, [yr, se], [tq])
                        stt(poi.ap[:, cc:cc + 1], yli, ce.ap[:, 511:512], tq.ap, ALU.mult, ALU.add, [yi, ce, tq], [poi], grp='poi')
                    ts('dve', tq.ap, yli, Rs, ALU.mult, [yi, pws], [tq])
                    stt(carr.ap[:, cc:cc + 1], ylr, Rc, tq.ap, ALU.mult, ALU.subtract, [yr, pwc, tq], [carr], grp=('car', cc))
                    ts('dve', tq.ap, ylr, Rs, ALU.mult, [yr, pws], [tq])
                    stt(cari.ap[:, cc:cc + 1], yli, Rc, tq.ap, ALU.mult, ALU.add, [yi, pwc, tq], [cari], grp=('cai', cc))
                    FA.off -= 16
                    tick()
                    tt('pool', p1.ap, yr.ap, ce.ap, ALU.mult, [yr, ce], [p1])
                    tt('pool', p2.ap, yi.ap, se.ap, ALU.mult, [yi, se], [p2])
                    tt('pool', p3.ap, yr.ap, se.ap, ALU.mult, [yr, se], [p3])
                    tt('pool', p4.ap, yi.ap, ce.ap, ALU.mult, [yi, ce], [p4])
                    tick()
                else:
                    ta, tb_ = wkk[2], wkk[3]
                    a_r = ar.ap[:, cc:cc + 1]; a_i = ai.ap[:, cc:cc + 1]; na_i = nai.ap[:, cc:cc + 1]
                    ts('dve', ta.ap[:, 0:n], x0r.ap[:, cc, :], a_r, ALU.mult, [x0r, ar], [ta])
                    stt(ta.ap[:, 0:n], x0i.ap[:, cc, :], na_i, ta.ap[:, 0:n], ALU.mult, ALU.add, [x0i, nai, ta], [ta])
                    tt('dve', sor.ap[:, cc, :], ta.ap[:, 0:n], psr.ap[:, 0:n], ALU.add, [ta, psr], [sor], grp='sor')
                    ts('dve', tb_.ap[:, 0:n], x0i.ap[:, cc, :], a_r, ALU.mult, [x0i, ar], [tb_])
                    stt(tb_.ap[:, 0:n], x0r.ap[:, cc, :], a_i, tb_.ap[:, 0:n], ALU.mult, ALU.add, [x0r, ai, tb_], [tb_])
                    tt('dve', soi.ap[:, cc, :], tb_.ap[:, 0:n], psi.ap[:, 0:n], ALU.add, [tb_, psi], [soi], grp='soi')
                    cp('act', p1.ap[:, 0:n], sor.ap[:, cc, :], [sor], [p1])
                    cp('act', p4.ap[:, 0:n], soi.ap[:, cc, :], [soi], [p4])
                py = psfix({0: 0, 512: 1, STK: 2}[c0])
                mm(py.ap[:, 0:n], CTr.ap[:, cc, :], p1.ap[:, 0:n], cc % 4 == 0, False, [CTr, p1], [py])
                if not smp:
                    mm(py.ap[:, 0:n], nCTr.ap[:, cc, :], p2.ap[:, 0:n], False, False, [nCTr, p2], [py])
                    mm(py.ap[:, 0:n], nCTi.ap[:, cc, :], p3.ap[:, 0:n], False, False, [nCTi, p3], [py])
                mm(py.ap[:, 0:n], nCTi.ap[:, cc, :], p4.ap[:, 0:n], False, cc % 4 == 3, [nCTi, p4], [py])
                if cc % 4 == 3:
                    stt(zf.ap[:, uc, c0:c0 + n], uf.ap[:, uc, c0:c0 + n], t_s5d.ap[:, j, uc:uc + 1], py.ap[:, 0:n], ALU.mult, ALU.add,
                        [uf, t_s5d, py], [zf], grp=('zf', c0))
            for _ in nxtg:
                pass
        if s == 1:
            dma(o_ss5r[j], sor.ap, rd=[sor]); dma(o_ss5i[j], soi.ap, rd=[soi])
            dma(o_ps5r[j], por.ap, rd=[por]); dma(o_ps5i[j], poi.ap, rd=[poi])
        for (c0, n, smp) in subtiles(s):
            gelu(zf.ap[:, :, c0:c0 + n], zf.ap[:, :, c0:c0 + n], [zf], [zf], grp=('zg', c0))
            cp('dve', zb.ap[:, :, c0:c0 + n], zf.ap[:, :, c0:c0 + n], [zf], [zb], grp=('zb', c0))
        for (c0, n, smp) in subtiles(s):
            def ev2(m, ps, c0=c0, n=n):
                sg = wk_[m % 2][0]
                act(sg.ap[:, 0:n], ps.ap[:, 0:n], AF.Sigmoid, [ps, t_bglu], [sg], bias=t_bglu.ap[:, j, m:m + 1])
                tt('dve', yab.ap[:, m, c0:c0 + n], zf.ap[:, m, c0:c0 + n], sg.ap[:, 0:n], ALU.mult, [zf, sg], [yab], grp=('ya', c0))
            linear(wg, wg.ap, 0, 4, 4, lambda k: zb.ap[:, k, c0:c0 + n], [zb], n, ev2)
        P.barrier()
        FA.off = fmark
        BA.off = bm_keep
        wh = BA.alloc(KC, 2048)
        load_w(wh, wh.ap, w_ab_in[j][:, 512:2560], KC, 2048, gain=gm)
        S = hg_S[j]
        Sb = [BA.alloc(128) for _h in range(4)]
        for _h in range(4):
            cp('act', Sb[_h].ap, S[_h].ap, [S[_h]], [Sb[_h]])
        lbp = lb_p.ap[:, j, :]
        omlbp = omlb_p.ap[:, j, :]
        ones512 = FA.alloc(512)
        memset('dve', ones512, ones512.ap, 1.0)
        for (c0, n, smp) in subtiles(s):
            fm, bm = FA.off, BA.off
            if not smp:
                nblk = n // 128
                vtm = BA.alloc(nblk, 512)
                for blk in range(nblk):
                    ps = psum()
                    for k in range(KC):
                        mm(ps.ap, hb.ap[:, k, c0 + blk * 128:c0 + (blk + 1) * 128], wh.ap[:, k, 1024:1536], k == 0, k == KC - 1, [hb, wh], [ps])
                    cp('act', vtm.ap[:, blk, :], ps.ap, [ps], [vtm], grp='vtm')
                sgo_a = [FA.alloc(512) for _h in range(4)]
                Dc_a = [FA.alloc(16) for _h in range(4)]
                ob_all = FA.alloc(4, 512)
                Qt_a = [BA.alloc(512) for _h in range(4)]
                Kt_a = [BA.alloc(512) for _h in range(4)]
                Kh_a = [BA.alloc(512) for _h in range(4)]
                khm_a = [BA.alloc(4, 128) for _h in range(4)]
                atm_a = [BA.alloc(128) for _h in range(4)]
                qs = FA.alloc(512); f_ = FA.alloc(512); lf = FA.alloc(512); k_ = FA.alloc(512); G = FA.alloc(512)
                A1 = FA.alloc(512); A2 = FA.alloc(512)
                Gs = FA.alloc(16)
                Ep = FA.alloc(512); Em = FA.alloc(512); Eh = FA.alloc(512)
                for h in range(4):
                    sgo = sgo_a[h]; Dc = Dc_a[h]; Qt = Qt_a[h]; Kt = Kt_a[h]; Kh = Kh_a[h]
                    psq = psum(); psf = psum(); psg = psum()
                    for k in range(KC):
                        mm(psq.ap[:, 0:n], wh.ap[:, k, h * 128:(h + 1) * 128], hb.ap[:, k, c0:c0 + n], k == 0, k == KC - 1, [wh, hb], [psq])
                    for k in range(KC):
                        mm(psf.ap[:, 0:n], wh.ap[:, k, 512 + h * 128:512 + (h + 1) * 128], hb.ap[:, k, c0:c0 + n], k == 0, k == KC - 1, [wh, hb], [psf])
                    for k in range(KC):
                        mm(psg.ap[:, 0:n], wh.ap[:, k, 1536 + h * 128:1536 + (h + 1) * 128], hb.ap[:, k, c0:c0 + n], k == 0, k == KC - 1, [wh, hb], [psg])
                    act(qs.ap, psq.ap, AF.Sigmoid, [psq], [qs])
                    act(sgo.ap, psg.ap, AF.Sigmoid, [psg], [sgo])
                    act(f_.ap, psf.ap, AF.Sigmoid, [psf], [f_])
                    tt('dve', qs.ap, qs.ap, psq.ap, ALU.mult, [qs, psq], [qs])
                    ts('dve', k_.ap, f_.ap, omlbp[:, h:h + 1], ALU.mult, [f_, omlb_p], [k_], s2=-1.0, op1=ALU.mult)
                    ts('dve', k_.ap, k_.ap, omlbp[:, h:h + 1], ALU.add, [k_, omlb_p], [k_])
                    ts('dve', f_.ap, f_.ap, omlbp[:, h:h + 1], ALU.mult, [f_, omlb_p, lb_p], [f_], s2=lbp[:, h:h + 1], op1=ALU.add)
                    act(lf.ap, f_.ap, AF.Ln, [f_], [lf])
                    P.op('dve', lambda e, o=G.ap, d1=lf.ap: e.tensor_tensor_scan(out=o, data0=ones512.ap, data1=d1, initial=0.0, op0=ALU.mult, op1=ALU.add),
                         [lf, ones512], [G])
                    G3 = G.ap.rearrange("p (a b) -> p a b", b=32)
                    memset('dve', Gs, Gs.ap[:, 0:1], 0.0)
                    cp('dve', Gs.ap[:, 1:16], G3[:, 0:15, 31], [G], [Gs], grp='gs')
                    tt('dve', A1.ap.rearrange("p (a b) -> p a b", b=32), G3, Gs.ap.unsqueeze(2).to_broadcast([128, 16, 32]), ALU.subtract, [G, Gs], [A1])
                    tt('dve', A2.ap.rearrange("p (a b) -> p a b", b=32), G3, G3[:, :, 31:32].to_broadcast([128, 16, 32]), ALU.subtract, [G], [A2])
                    tt('dve', Dc.ap, G3[:, :, 31], Gs.ap, ALU.subtract, [G, Gs], [Dc])
                    act(Dc.ap, Dc.ap, AF.Exp, [Dc], [Dc])
                    act(Ep.ap, A1.ap, AF.Exp, [A1], [Ep])
                    act(Em.ap, A1.ap, AF.Exp, [A1], [Em], scale=-1.0)
                    act(Eh.ap, A2.ap, AF.Exp, [A2], [Eh], scale=-1.0)
                    tt('pool', Qt.ap, qs.ap, Ep.ap, ALU.mult, [qs, Ep], [Qt])
                    tt('pool', Kt.ap, k_.ap, Em.ap, ALU.mult, [k_, Em], [Kt])
                    tt('pool', Kh.ap, k_.ap, Eh.ap, ALU.mult, [k_, Eh], [Kh])
                for blk in range(nblk):
                    b0 = blk * 128
                    pso = psfix(0)
                    psn = psfix(1)
                    for h in range(4):
                        Qt = Qt_a[h]; Kt = Kt_a[h]; Kh = Kh_a[h]; khm = khm_a[h]; atm = atm_a[h]
                        psa = psum()
                        mm(psa.ap[:, 0:128], Kt.ap[:, b0:b0 + 128], Qt.ap[:, b0:b0 + 128], True, True, [Kt, Qt], [psa])
                        tt('dve', atm.ap, psa.ap[:, 0:128], mask16.ap, ALU.mult, [psa, mask16], [atm])
                        P.op('pe', lambda e, o=psbf.ap[:, 512:640], i=Kh.ap[:, b0:b0 + 128]: e.transpose(out=o, in_=i, identity=ident_bf.ap), [Kh, ident_bf], [psbf])
                        tt('dve', khm.ap, psbf.ap[:, 512:640].unsqueeze(1).to_broadcast([128, 4, 128]), cmask.ap.unsqueeze(2).to_broadcast([128, 4, 128]), ALU.mult, [psbf, cmask], [khm])
                        mm(pso.ap[:, h * 128:(h + 1) * 128], vtm.ap[:, blk, h * 128:(h + 1) * 128], atm.ap, True, True, [vtm, atm], [pso])
                    for c in range(4):
                        for h in range(4):
                            mm(psn.ap[:, h * 128 + 32 * c:h * 128 + 32 * c + 32], Sb[h].ap, Qt_a[h].ap[:, b0 + 32 * c:b0 + 32 * c + 32], True, True, [Sb[h], Qt_a[h]], [psn])
                            psu = psum()
                            mm(psu.ap[:, 0:128], khm_a[h].ap[:, c, :], vtm.ap[:, blk, h * 128:(h + 1) * 128], True, True, [khm_a[h], vtm], [psu])
                            stt(S[h].ap, S[h].ap, Dc_a[h].ap[:, blk * 4 + c:blk * 4 + c + 1], psu.ap[:, 0:128], ALU.mult, ALU.add, [S[h], Dc_a[h], psu], [S[h]])
                            cp('act', Sb[h].ap, S[h].ap, [S[h]], [Sb[h]])
                    cp('act', ob_all.ap[:, :, b0:b0 + 128], pso.ap.rearrange("p (a b) -> p a b", a=4), [pso], [ob_all], grp='ob')
                    tt('dve', ob_all.ap[:, :, b0:b0 + 128], ob_all.ap[:, :, b0:b0 + 128], psn.ap.rearrange("p (a b) -> p a b", a=4), ALU.add, [ob_all, psn], [ob_all], grp='ob2')
                for h in range(4):
                    fm2, bm2 = FA.off, BA.off
                    rs = rstd_of(lambda c, h=h: ob_all.ap[:, h, 0:n], 1, 0, n, 1.0 / 128, [ob_all])
                    tmpo = FA.alloc(512)
                    tt('dve', tmpo.ap, ob_all.ap[:, h, :], rs.ap[:, 0:n], ALU.mult, [ob_all, rs], [tmpo])
                    stt(yab.ap[:, 4 + h, c0:c0 + n], tmpo.ap, t_gnp.ap[:, j, h:h + 1], sgo_a[h].ap, ALU.mult, ALU.mult, [tmpo, t_gnp, sgo_a[h]], [yab], grp=('yb', c0))
                    FA.off, BA.off = fm2, bm2
                if s == 1 and c0 == 512:
                    for _h in range(4):
                        dma(o_phg[j, _h], S[_h].ap, rd=[S[_h]])
            else:
                lbb = FA.alloc(512)
                dma(lbb.ap[0:NS, :], lbl_b[1:2, :].to_broadcast([NS, 512]) if False else lbl_b[1, :].partition_broadcast(NS), wr=[lbb])
                lb0 = FA.alloc(512)
                dma(lb0.ap[0:NS, :], lbl_b[0, :].partition_broadcast(NS), wr=[lb0])
                psft = psum(); psvt = psum()
                for k in range(KC):
                    mm(psft.ap[0:NS, :], hb.ap[:, k, c0:c0 + n], wh.ap[:, k, 512:1024], k == 0, k == KC - 1, [hb, wh], [psft])
                for k in range(KC):
                    mm(psvt.ap[0:NS, :], hb.ap[:, k, c0:c0 + n], wh.ap[:, k, 1024:1536], k == 0, k == KC - 1, [hb, wh], [psvt])
                ktm = FA.alloc(512); vt_ = FA.alloc(512); lbt = FA.alloc(512)
                if j == 0:
                    memset('dve', lbt, lbt.ap[0:NS, :], 0.0)
                else:
                    tt('dve', lbt.ap[0:NS, :], lbb.ap[0:NS, :], lb0.ap[0:NS, :], ALU.subtract, [lbb, lb0], [lbt])
                    act(lbt.ap[0:NS, :], lbt.ap[0:NS, :], AF.Sigmoid, [lbt], [lbt])
                act(ktm.ap[0:NS, :], psft.ap[0:NS, :], AF.Sigmoid, [psft], [ktm])
                ts('dve', ktm.ap[0:NS, :], ktm.ap[0:NS, :], -1.0, ALU.mult, [ktm], [ktm], s2=1.0, op1=ALU.add)
                ts('dve', lbt.ap[0:NS, :], lbt.ap[0:NS, :], -1.0, ALU.mult, [lbt], [lbt], s2=1.0, op1=ALU.add)
                tt('dve', ktm.ap[0:NS, :], ktm.ap[0:NS, :], lbt.ap[0:NS, :], ALU.mult, [ktm, lbt], [ktm])
                cp('act', vt_.ap[0:NS, :], psvt.ap[0:NS, :], [psvt], [vt_])
                kms = [FA.alloc(512) for _ in range(2)]
                qs = FA.alloc(4, NS); f_ = FA.alloc(4, NS); sgo = FA.alloc(4, NS)
                for h in range(4):
                    psq = psum(); psf = psum(); psg = psum()
                    for k in range(KC):
                        mm(psq.ap[:, 0:n], wh.ap[:, k, h * 128:(h + 1) * 128], hb.ap[:, k, c0:c0 + n], k == 0, k == KC - 1, [wh, hb], [psq])
                    for k in range(KC):
                        mm(psf.ap[:, 0:n], wh.ap[:, k, 512 + h * 128:512 + (h + 1) * 128], hb.ap[:, k, c0:c0 + n], k == 0, k == KC - 1, [wh, hb], [psf])
                    for k in range(KC):
                        mm(psg.ap[:, 0:n], wh.ap[:, k, 1536 + h * 128:1536 + (h + 1) * 128], hb.ap[:, k, c0:c0 + n], k == 0, k == KC - 1, [wh, hb], [psg])
                    act(qs.ap[:, h, :], psq.ap[:, 0:n], AF.Silu, [psq], [qs], grp='qs')
                    act(sgo.ap[:, h, :], psg.ap[:, 0:n], AF.Sigmoid, [psg], [sgo], grp='sgo')
                    act(f_.ap[:, h, :], psf.ap[:, 0:n], AF.Sigmoid, [psf], [f_], grp='f1')
                    ts('dve', f_.ap[:, h, :], f_.ap[:, h, :], omlbp[:, h:h + 1], ALU.mult, [f_, omlb_p, lb_p], [f_], s2=lbp[:, h:h + 1], op1=ALU.add, grp='f2')
                s0s = [FA.alloc(4, 128) for _ in range(2)]
                sns = [FA.alloc(4, 128) for _ in range(2)]
                pso = psfix(2)
                for nn in range(NS):
                    s0 = s0s[nn % 2]
                    sn = sns[nn % 2]
                    dma(s0.ap, hgs[j, nn].rearrange("h d v -> d h v"), wr=[s0])
                    psu = psum()
                    km = kms[nn % 2]
                    ts('dve', km.ap[0:NS, :], ktm.ap[0:NS, :], ident.ap[0:NS, nn:nn + 1], ALU.mult, [ktm, ident], [km])
                    for h in range(4):
                        mm(psu.ap[:, h * 128:(h + 1) * 128], km.ap[0:NS, h * 128:(h + 1) * 128], vt_.ap[0:NS, h * 128:(h + 1) * 128], True, True, [km, vt_], [psu])
                    for h in range(4):
                        stt(sn.ap[:, h, :], s0.ap[:, h, :], f_.ap[:, h, nn:nn + 1], psu.ap[:, h * 128:(h + 1) * 128], ALU.mult, ALU.add, [s0, f_, psu], [sn], grp=('sn', nn))
                    dma(o_shg[j, nn].rearrange("h d v -> d h v"), sn.ap, rd=[sn])
                    for h in range(4):
                        mm(pso.ap[:, h * NS + nn:h * NS + nn + 1], sn.ap[:, h, :], qs.ap[:, h, nn:nn + 1], True, True, [sn, qs], [pso])
                ob = FA.alloc(4, NS)
                cp('act', ob.ap, pso.ap[:, 0:4 * NS].rearrange("p (a b) -> p a b", a=4), [pso], [ob])
                for h in range(4):
                    rs = rstd_of(lambda c: ob.ap[:, h, :], 1, 0, n, 1.0 / 128, [ob])
                    tt('pool', ob.ap[:, h, :], ob.ap[:, h, :], rs.ap[:, 0:n], ALU.mult, [ob, rs], [ob])
                    stt(yab.ap[:, 4 + h, c0:c0 + n], ob.ap[:, h, :], t_gnp.ap[:, j, h:h + 1], sgo.ap[:, h, :], ALU.mult, ALU.mult, [ob, t_gnp, sgo], [yab], grp=('yb', c0))
            FA.off, BA.off = fm, bm
        P.barrier()
        FA.off = fmark
        BA.off = bm_keep
        wo = BA.alloc(KC, D)
        load_w(wo, wo.ap, w_ab_out[j], KC, D)
        yb = FA.alloc(KC, W)
        for (c0, n, smp) in subtiles(s):
            linear(wo, wo.ap, 0, KC, KC, lambda k: yab.ap[:, k, c0:c0 + n], [yab], n,
                   lambda m, ps: cp('act', yb.ap[:, m, c0:c0 + n], ps.ap[:, 0:n], [ps], [yb], grp=('yb', c0)))
        postnorm(s, yb, li, 1)
        reset()

    mem_prep()
    stop = (STOP_AFTER == 0)
    for s in range(NSUP):
        if stop:
            break
        dma(x.ap[:, :, 0:STK], xT.rearrange("(c p) t -> p c t", p=128)[:, :, s * STK:(s + 1) * STK], wr=[x])
        if s == 1:
            dma(x.ap[:, :, STK:W], xsT.rearrange("(c p) t -> p c t", p=128), wr=[x], grp='xs')
        for li in range(DEPTH):
            if li % 2 == 0:
                ab(s, li)
            else:
                conf(s, li)
            if done():
                stop = True
                break
            xattn(s, li)
            if done():
                stop = True
                break
            ffn(s, li)
            if done():
                stop = True
                break
        dma(yT.rearrange("(c p) t -> p c t", p=128)[:, :, s * STK:(s + 1) * STK], x.ap[:, :, 0:STK], rd=[x])
        if s == 1:
            dma(ysT.rearrange("(c p) t -> p c t", p=128), x.ap[:, :, STK:W], rd=[x])

    blk = enter(nc.Block())
    P.emit(nc, blk, sems, dsems)
    for c in reversed(ctx):
        c.__exit__(None, None, None)
    print("arena peak f32", FA.peak, "bf16", BA.peak, "ops", {e: len(P.ops[e]) for e in ENGS})
    return nc


def _pad_lhsT(a, mode):
    out = np.zeros((128, 16, 128), np.float32)
    for g in range(32):
        cc, two = g // 2, g % 2
        ccl = cc % 4
        blk = a[g] if mode == 'b' else a[g].T
        out[two * 64:(two + 1) * 64, cc, ccl * 32 + two * 16: ccl * 32 + two * 16 + 16] = blk
    return out


_NC_CACHE = {}


def kernel(**inp):
    f = lambda a: np.ascontiguousarray(np.asarray(a, dtype=np.float32))
    if 'nc' not in _NC_CACHE:
        _NC_CACHE['nc'] = build()
    nc = _NC_CACHE['nc']
    shared = {}
    for k in ['w_ab_in', 'w_ab_out', 'w_conf_in', 'w_conf_out', 'w_xq', 'w_xk', 'w_xv', 'w_xo', 'w_ffn_in', 'w_ffn_out']:
        shared[k] = f(inp[k])
    shared['w_glu'] = f(inp['s5_w_glu'])
    shared['gains'] = f(np.asarray(inp['norm_gains']).reshape(DEPTH, 7, 8, 128).transpose(3, 0, 1, 2))

    def pl(a):
        return f(np.asarray(a).reshape(2, 16, 2, 64).transpose(0, 2, 3, 1).reshape(2, 128, 16))
    shared['lamr_p'] = pl(inp['s5_lambda_re'])
    shared['lami_p'] = pl(inp['s5_lambda_im'])
    shared['ldt_p'] = pl(np.repeat(np.asarray(inp['s5_log_dt'])[:, :, None], 64, axis=2))
    shared['brN'] = f(np.stack([_pad_lhsT(np.asarray(inp['s5_b_re'])[j], 'b') for j in range(2)]))
    shared['biN'] = f(np.stack([_pad_lhsT(np.asarray(inp['s5_b_im'])[j], 'b') for j in range(2)]))
    shared['crN'] = f(np.stack([_pad_lhsT(np.asarray(inp['s5_c_re'])[j], 'c') for j in range(2)]))
    shared['ciN'] = f(np.stack([_pad_lhsT(np.asarray(inp['s5_c_im'])[j], 'c') for j in range(2)]))

    def pc(a, nch):
        a = np.asarray(a)
        return f(a.reshape(a.shape[0], nch, 128).transpose(2, 0, 1))
    shared['s5d'] = pc(inp['s5_d'], 4)
    shared['bglu'] = pc(inp['s5_b_glu'], 4)
    shared['lbl_p'] = pc(inp['hg_lb_logits'], 4)
    shared['lbl_b'] = f(inp['hg_lb_logits'])
    shared['gnp'] = pc(inp['hg_gnorm'], 4)
    shared['cdw'] = f(np.asarray(inp['conf_dw']).reshape(2, 31, 8, 128).transpose(3, 0, 2, 1))
    shared['cdb'] = pc(inp['conf_dw_b'], 8)
    shared['clg'] = pc(inp['conf_ln_g'], 8)
    shared['clb'] = pc(inp['conf_ln_b'], 8)
    shared['fdw'] = f(np.asarray(inp['ffn_dw']).reshape(DEPTH, 3, FC, 128).transpose(3, 0, 2, 1))
    shared['fdb'] = pc(inp['ffn_dw_b'], FC)
    shared['c_ident'] = np.eye(128, dtype=np.float32)
    ss_, tt_ = np.meshgrid(np.arange(128), np.arange(128), indexing='ij')
    shared['c_mask'] = ((tt_ >= ss_) & (tt_ // 32 == ss_ // 32)).astype(np.float32)
    shared['c_cmask'] = (np.arange(128)[:, None] // 32 == np.arange(4)[None, :]).astype(np.float32)

    xp = np.asarray(inp['x_prompt']); xs = np.asarray(inp['x_sample']); mp = np.asarray(inp['mem_prompt'])
    ck = np.asarray(inp['cache_mem_k']); cv = np.asarray(inp['cache_mem_v'])
    sr = np.asarray(inp['state_s5_re']); si = np.asarray(inp['state_s5_im'])
    hg = np.asarray(inp['state_hgrn']); cf = np.asarray(inp['state_conf']); ff = np.asarray(inp['state_ffn'])
    in_maps = []
    for c in range(NCORE):
        sl = slice(c * NS, (c + 1) * NS)
        m = dict(shared)
        m['xT'] = f(xp[c].T)
        m['xsT'] = f(xs[sl, 0, :].T)
        m['memT'] = f(mp[c].T)
        m['kcT'] = f(ck[:, sl].transpose(0, 1, 3, 4, 2))
        m['vc'] = f(cv[:, sl].reshape(DEPTH, NS, 256, D))

        def s5l(a):
            return f(a[:, sl].reshape(2, NS, 16, 2, 64).transpose(0, 3, 4, 2, 1).reshape(2, 128, 16, NS))
        m['s5r'] = s5l(sr)
        m['s5i'] = s5l(si)
        m['hgs'] = f(hg[:, sl])
        m['cfs'] = f(cf[:, sl].reshape(2, NS, 30, 8, 128).transpose(0, 4, 3, 1, 2))
        m['ffs'] = f(ff[:, sl].reshape(DEPTH, NS, 2, FC, 128).transpose(0, 4, 3, 1, 2))
        in_maps.append(m)
    res = run_bass_kernel_spmd(nc, in_maps[:NRUN], core_ids=list(range(NRUN)))
    R = res.results
    g = lambda name: np.stack([np.asarray(R[min(c, NRUN - 1)][name]) for c in range(NCORE)])
    y_prompt = g('yT').transpose(0, 2, 1)
    y_sample = g('ysT').transpose(0, 2, 1).reshape(NCORE * NS, 1, D)

    def ps5(a):
        return a.reshape(NCORE, 2, 2, 64, 16).transpose(1, 0, 4, 2, 3).reshape(2, NCORE, 32, 64)
    p_re = ps5(g('o_ps5r')); p_im = ps5(g('o_ps5i'))
    p_hg = g('o_phg').transpose(1, 0, 2, 3, 4)
    p_conf = g('o_pconf').transpose(1, 0, 4, 3, 2).reshape(2, NCORE, 30, D)
    p_ffn = g('o_pffn').transpose(1, 0, 4, 3, 2).reshape(DEPTH, NCORE, 2, DFF)
    p_mk = g('o_pmk').transpose(1, 0, 2, 3).reshape(DEPTH, NCORE, NMEM, 4, 256)
    p_mv = g('o_pmv').transpose(1, 0, 2, 3).reshape(DEPTH, NCORE, NMEM, 4, 256)

    def ss5(a):
        return a.reshape(NCORE, 2, 2, 64, 16, NS).transpose(1, 0, 5, 4, 2, 3).reshape(2, NCORE * NS, 32, 64)
    s_re = ss5(g('o_ss5r')); s_im = ss5(g('o_ss5i'))
    s_hg = g('o_shg').transpose(1, 0, 2, 3, 4, 5).reshape(2, NCORE * NS, 4, 128, 128)
    s_conf = g('o_sconf').transpose(1, 0, 4, 5, 3, 2).reshape(2, NCORE * NS, 30, D)
    s_ffn = g('o_sffn').transpose(1, 0, 4, 5, 3, 2).reshape(DEPTH, NCORE * NS, 2, DFF)
    outs = (y_prompt, y_sample, p_re, p_im, p_hg, p_conf, p_ffn, p_mk, p_mv, s_re, s_im, s_hg, s_conf, s_ffn)
    return tuple(np.ascontiguousarray(o, dtype=np.float32) for o in outs)
```
